# Optimizing a Trainium2 kernel written in Bass

```python
import math
import jax, jax.numpy as jnp
from jax import lax
import numpy as np

D_MODEL = 1024
BATCH = 8
SEQ = 2048
DEPTH = 1

N_META = 16
BLOCK = 128
WINDOW = 128
PAD = (-N_META) % BLOCK
NORM_EPS = 1e-6

ATT_HEAD_DIM = 64
ATT_Q_HEADS = D_MODEL // ATT_HEAD_DIM
ATT_KV_HEADS = 4
ATT_GROUP = ATT_Q_HEADS // ATT_KV_HEADS
ATT_WIDTH = ATT_Q_HEADS * ATT_HEAD_DIM
KV_WIDTH = ATT_KV_HEADS * ATT_HEAD_DIM

SSM_INNER = 2 * D_MODEL
SSM_HEAD_DIM = 64
SSM_HEADS = SSM_INNER // SSM_HEAD_DIM
SSM_GROUPS = 4
SSM_HEADS_PER_GROUP = SSM_HEADS // SSM_GROUPS
SSM_STATE = 128
CONV_WIDTH = 4
CONV_DIM = SSM_INNER + 2 * SSM_GROUPS * SSM_STATE

SPLIT_SIZES = (ATT_WIDTH, KV_WIDTH, KV_WIDTH, ATT_WIDTH, SSM_INNER, CONV_DIM, SSM_HEADS, D_MODEL, D_MODEL)
SPLIT_POINTS = tuple(int(s) for s in np.cumsum(SPLIT_SIZES)[:-1])
IN_PROJ_DIM = int(sum(SPLIT_SIZES))

kernel_name = 'hybrid_swa_sink_alibi_ssd_gated_merge'


def rmsnorm(x, g):
    xf = x.astype(jnp.float32)
    y = xf * lax.rsqrt(jnp.mean(xf * xf, axis=-1, keepdims=True) + NORM_EPS) * g.astype(jnp.float32)
    return y.astype(x.dtype)


def alibi_slopes():
    return jnp.asarray(np.array([2.0 ** (-8.0 * (h + 1) / ATT_Q_HEADS) for h in range(ATT_Q_HEADS)], np.float32))


def sliding_window_attention(q, k, v, sinks):
    b, lp, _ = q.shape
    nb = lp // BLOCK
    km = k[:, PAD:PAD + N_META].reshape(b, N_META, ATT_KV_HEADS, ATT_HEAD_DIM)
    vm = v[:, PAD:PAD + N_META].reshape(b, N_META, ATT_KV_HEADS, ATT_HEAD_DIM)
    qb = q.reshape(b, nb, BLOCK, ATT_KV_HEADS, ATT_GROUP, ATT_HEAD_DIM) * (ATT_HEAD_DIM ** -0.5)

    def with_prev(t):
        t = t.reshape(b, nb, BLOCK, ATT_KV_HEADS, ATT_HEAD_DIM)
        prev = jnp.concatenate([jnp.zeros_like(t[:, :1]), t[:, :-1]], axis=1)
        return jnp.concatenate([prev, t], axis=2)

    kb, vb = with_prev(k), with_prev(v)

    q_pos = jnp.arange(nb)[:, None] * BLOCK + jnp.arange(BLOCK)[None, :]
    k_pos = jnp.arange(nb)[:, None] * BLOCK - BLOCK + jnp.arange(2 * BLOCK)[None, :]
    rel = q_pos[:, :, None] - k_pos[:, None, :]
    band_ok = (rel >= 0) & (rel < WINDOW) & (k_pos[:, None, :] >= PAD + N_META)
    meta_pos = PAD + jnp.arange(N_META)
    meta_ok = meta_pos[None, None, :] <= q_pos[:, :, None]

    slopes = alibi_slopes().reshape(1, 1, ATT_KV_HEADS, ATT_GROUP, 1, 1)
    s_band = jnp.einsum('bnqkgd,bnskd->bnkgqs', qb, kb).astype(jnp.float32)
    s_band = s_band - slopes * rel[None, :, None, None].astype(jnp.float32)
    s_band = jnp.where(band_ok[None, :, None, None], s_band, -jnp.inf)
    s_meta = jnp.einsum('bnqkgd,bmkd->bnkgqm', qb, km).astype(jnp.float32)
    s_meta = jnp.where(meta_ok[None, :, None, None], s_meta, -jnp.inf)
    sink = jnp.broadcast_to(sinks.astype(jnp.float32).reshape(1, 1, ATT_KV_HEADS, ATT_GROUP, 1, 1),
                            s_meta.shape[:-1] + (1,))
    p = jax.nn.softmax(jnp.concatenate([sink, s_meta, s_band], axis=-1), axis=-1).astype(v.dtype)
    p_meta, p_band = p[..., 1:1 + N_META], p[..., 1 + N_META:]
    o = jnp.einsum('bnkgqm,bmkd->bnqkgd', p_meta, vm) + jnp.einsum('bnkgqs,bnskd->bnqkgd', p_band, vb)
    return o.reshape(b, lp, ATT_WIDTH)


def causal_depthwise_conv(u, w, bias):
    out = lax.conv_general_dilated(u, w[:, None, :].astype(u.dtype), window_strides=(1,),
                                   padding=[(CONV_WIDTH - 1, 0)],
                                   dimension_numbers=('NWC', 'WIO', 'NWC'),
                                   feature_group_count=u.shape[-1])
    return out + bias.astype(u.dtype)


def segsum(a):
    T = a.shape[-1]
    cs = jnp.cumsum(a, axis=-1)
    diff = cs[..., :, None] - cs[..., None, :]
    return jnp.where(jnp.tril(jnp.ones((T, T), bool)), diff, -jnp.inf)


def ssd_chunked(x, dt, A, Bm, Cm):
    b, L = x.shape[:2]
    nc = L // BLOCK
    G, R, P, N = SSM_GROUPS, SSM_HEADS_PER_GROUP, SSM_HEAD_DIM, SSM_STATE
    xr = (x * dt[..., None]).reshape(b, nc, BLOCK, G, R, P)
    a = (dt * A).reshape(b, nc, BLOCK, G, R).transpose(0, 1, 3, 4, 2)
    Br = Bm.reshape(b, nc, BLOCK, G, N)
    Cr = Cm.reshape(b, nc, BLOCK, G, N)
    a_cs = jnp.cumsum(a, axis=-1)
    decay = jnp.exp(segsum(a))
    cb = jnp.einsum('bclgn,bcsgn->bcgls', Cr, Br)
    y_diag = jnp.einsum('bcgls,bcgrls,bcsgrp->bclgrp', cb, decay, xr)
    decay_states = jnp.exp(a_cs[..., -1:] - a_cs)
    states = jnp.einsum('bclgn,bcgrl,bclgrp->bcgrpn', Br, decay_states, xr)
    chunk_decay = jnp.exp(a_cs[..., -1])

    def step(carry, inp):
        s_c, d_c = inp
        return carry * d_c[..., None, None] + s_c, carry

    init = jnp.zeros((b, G, R, P, N), jnp.float32)
    _, prev = lax.scan(step, init, (jnp.moveaxis(states, 1, 0), jnp.moveaxis(chunk_decay, 1, 0)))
    prev = jnp.moveaxis(prev, 0, 1)
    y_off = jnp.einsum('bclgn,bcgrpn,bcgrl->bclgrp', Cr, prev, jnp.exp(a_cs))
    return (y_diag + y_off).reshape(b, L, SSM_HEADS, P)


def ssd_branch(z, xbc, dt_raw, conv_w, conv_b, dt_bias, a_log, d_skip, g_norm, valid):
    b, L, _ = xbc.shape
    xbc = jax.nn.silu(causal_depthwise_conv(xbc, conv_w, conv_b)) * valid[None, :, None]
    xbc = xbc.astype(jnp.float32)
    xs = xbc[..., :SSM_INNER].reshape(b, L, SSM_HEADS, SSM_HEAD_DIM)
    Bm = xbc[..., SSM_INNER:SSM_INNER + SSM_GROUPS * SSM_STATE].reshape(b, L, SSM_GROUPS, SSM_STATE)
    Cm = xbc[..., SSM_INNER + SSM_GROUPS * SSM_STATE:].reshape(b, L, SSM_GROUPS, SSM_STATE)
    dt = jax.nn.softplus(dt_raw.astype(jnp.float32) + dt_bias.astype(jnp.float32))
    A = -jnp.exp(a_log.astype(jnp.float32))
    y = ssd_chunked(xs, dt, A, Bm, Cm) + d_skip.astype(jnp.float32)[:, None] * xs
    y = y.reshape(b, L, SSM_INNER) * jax.nn.silu(z.astype(jnp.float32))
    yg = y.reshape(b, L, SSM_GROUPS, SSM_INNER // SSM_GROUPS)
    yg = yg * lax.rsqrt(jnp.mean(yg * yg, axis=-1, keepdims=True) + NORM_EPS)
    y = yg.reshape(b, L, SSM_INNER) * g_norm.astype(jnp.float32)
    return y.astype(z.dtype)


def hybrid_layer(h, valid, g_pre, w_in, conv_w, conv_b, dt_bias, a_log, d_skip, attn_sinks,
                 g_ssm_norm, w_out_att, w_out_ssm, w_out, g_post):
    u = rmsnorm(h, g_pre)
    proj = u @ w_in
    q, k, v, z_att, z_ssm, xbc, dt_raw, gate_att, gate_ssm = jnp.split(proj, SPLIT_POINTS, axis=-1)
    y_att = (sliding_window_attention(q, k, v, attn_sinks) * jax.nn.silu(z_att)) @ w_out_att
    y_ssm = ssd_branch(z_ssm, xbc, dt_raw, conv_w, conv_b, dt_bias, a_log, d_skip, g_ssm_norm, valid) @ w_out_ssm
    merged = jax.nn.sigmoid(gate_att) * y_att + jax.nn.sigmoid(gate_ssm) * y_ssm
    out = merged @ w_out
    return h + rmsnorm(out, g_post) * valid[None, :, None]


def setup_inputs(seed: int = 0) -> dict:
    key = jax.random.key(seed)
    ks = jax.random.split(key, 16)
    f32 = jnp.float32

    def nrm(k, shape, scale):
        return jax.random.normal(k, shape, f32) * scale

    x = nrm(ks[0], (BATCH, SEQ, D_MODEL), 1.0)
    meta_tokens = nrm(ks[1], (N_META, D_MODEL), 1.0)
    g_pre = 1.0 + nrm(ks[2], (DEPTH, D_MODEL), 0.01)
    w_in = nrm(ks[3], (DEPTH, D_MODEL, IN_PROJ_DIM), D_MODEL ** -0.5)
    conv_w = nrm(ks[4], (DEPTH, CONV_WIDTH, CONV_DIM), CONV_WIDTH ** -0.5)
    conv_b = nrm(ks[5], (DEPTH, CONV_DIM), 0.02)
    dt0 = jnp.exp(jax.random.uniform(ks[6], (DEPTH, SSM_HEADS), f32, math.log(1e-3), math.log(1e-1)))
    dt_bias = dt0 + jnp.log(-jnp.expm1(-dt0))
    a_log = jnp.log(jax.random.uniform(ks[7], (DEPTH, SSM_HEADS), f32, 1.0, 16.0))
    d_skip = 1.0 + nrm(ks[8], (DEPTH, SSM_HEADS), 0.1)
    attn_sinks = nrm(ks[9], (DEPTH, ATT_Q_HEADS), 0.5)
    g_ssm_norm = 1.0 + nrm(ks[10], (DEPTH, SSM_INNER), 0.01)
    w_out_att = nrm(ks[11], (DEPTH, ATT_WIDTH, D_MODEL), ATT_WIDTH ** -0.5)
    w_out_ssm = nrm(ks[12], (DEPTH, SSM_INNER, D_MODEL), SSM_INNER ** -0.5)
    w_out = nrm(ks[13], (DEPTH, D_MODEL, D_MODEL), D_MODEL ** -0.5)
    g_post = 1.0 + nrm(ks[14], (DEPTH, D_MODEL), 0.01)
    return {'x': x, 'meta_tokens': meta_tokens, 'g_pre': g_pre, 'w_in': w_in, 'conv_w': conv_w,
            'conv_b': conv_b, 'dt_bias': dt_bias, 'a_log': a_log, 'd_skip': d_skip,
            'attn_sinks': attn_sinks, 'g_ssm_norm': g_ssm_norm, 'w_out_att': w_out_att,
            'w_out_ssm': w_out_ssm, 'w_out': w_out, 'g_post': g_post}


def reference(x, meta_tokens, g_pre, w_in, conv_w, conv_b, dt_bias, a_log, d_skip, attn_sinks,
              g_ssm_norm, w_out_att, w_out_ssm, w_out, g_post):
    b = x.shape[0]
    lp = PAD + N_META + x.shape[1]
    h = jnp.concatenate([jnp.zeros((b, PAD, D_MODEL), x.dtype),
                         jnp.broadcast_to(meta_tokens.astype(x.dtype)[None], (b, N_META, D_MODEL)),
                         x], axis=1)
    valid = (jnp.arange(lp) >= PAD).astype(x.dtype)
    for i in range(DEPTH):
        h = hybrid_layer(h, valid, g_pre[i], w_in[i], conv_w[i], conv_b[i], dt_bias[i], a_log[i],
                         d_skip[i], attn_sinks[i], g_ssm_norm[i], w_out_att[i], w_out_ssm[i],
                         w_out[i], g_post[i])
    return h[:, PAD + N_META:]
```

```python
import numpy as np
from contextlib import ExitStack
import concourse.bass as bass
import concourse.mybir as mybir
from concourse.bass_utils import run_bass_kernel_spmd

F32 = mybir.dt.float32
BF16 = mybir.dt.bfloat16
AF = mybir.ActivationFunctionType
ALU = mybir.AluOpType

D = 1024
SEQ = 2048
NB = 17
EPS = 1e-6
BIG = 3.0e38
NCORES = 8


class Res:
    __slots__ = ("w", "r", "x")

    def __init__(self, excl=False):
        self.w = None
        self.r = {}
        self.x = excl


class Buf:
    def __init__(self, t, excl=False):
        self.t = t
        self.res = {}
        self.excl = excl

    def r(self, *keys):
        out = []
        for k in keys:
            if k not in self.res:
                self.res[k] = Res(self.excl)
            out.append(self.res[k])
        return out

    def __getitem__(self, k):
        return self.t[k]


class Eng:
    def __init__(self, e, name, sem):
        self.e, self.name, self.sem = e, name, sem
        self.count = 0
        self.seen = {}


class _Proxy:
    def __init__(self):
        self.call = None

    def __getattr__(self, name):
        def f(*a, **k):
            self.call = (name, a, k)
            return None
        return f


def _fsz(ap):
    n = 1
    for d in ap.shape[1:]:
        n *= d
    return n


class T:
    def __init__(self, nc, es):
        self.nc = nc
        self.es = es
        mk = lambda n: es.enter_context(nc.semaphore(n))
        self.pe = Eng(nc.tensor, "pe", mk("s_pe"))
        self.act = Eng(nc.scalar, "act", mk("s_act"))
        self.dve = Eng(nc.vector, "dve", mk("s_dve"))
        self.pool = Eng(nc.gpsimd, "pool", mk("s_pool"))
        self.sp = Eng(nc.sync, "sp", mk("s_sp"))
        self.KQ = 12
        self.qsems = {"sp": [mk(f"dq_sp{i}") for i in range(self.KQ)],
                      "pool": [mk(f"dq_pl{i}") for i in range(self.KQ)]}
        self.qn = {"sp": 0, "pool": 0}
        self.out_tokens = []
        self.ops = []

    def op(self, eng, fn, reads=(), writes=(), signal=True):
        p = _Proxy()
        fn(p)
        name, a, k = p.call
        self.ops.append(dict(kind=0, eng=eng, name=name, a=a, k=k, reads=list(reads), writes=list(writes), signal=signal))

    def dma(self, q, out, in_, reads=(), writes=(), is_output=False):
        eng = self.sp if q == "sp" else self.pool
        self.ops.append(dict(kind=1, eng=eng, q=q, out=out, in_=in_, reads=list(reads), writes=list(writes),
                             is_output=is_output, signal=True))

    def _dur(self, o):
        if o["kind"] == 1:
            ap = o["out"]
            nbytes = ap.shape[0] * _fsz(ap) * 4
            return 100.0, 2000.0 + nbytes / 280.0
        name, a, k = o["name"], o["a"], o["k"]
        en = o["eng"].name
        if en == "pe":
            if name == "transpose":
                n, f32 = 128, False
            else:
                n = _fsz(k["rhs"])
                f32 = (k["lhsT"].dtype == F32)
            d = 56.0 + max(n, 64) * 0.41 * (4 if f32 else 1)
            return d, d
        out = k.get("out", a[0] if a else None)
        n = _fsz(out) if out is not None else 64
        if en == "act":
            d = (n + 224) / 1.2 * 1.25
        else:
            two = name in ("tensor_tensor", "scalar_tensor_tensor")
            f = 1.0 if two else 0.5
            d = (n * f + 64) / 0.96 * 1.45
            if en == "pool":
                d *= 1.95
        return d, d

    def flush(self, do_schedule=True):
        import heapq
        ops = self.ops
        n = len(ops)
        atom_of = list(range(n))
        open_atom = None
        for i, o in enumerate(ops):
            if o["kind"] == 0 and o["eng"].name == "pe":
                if open_atom is None:
                    open_atom = i
                atom_of[i] = open_atom
                if o["signal"]:
                    open_atom = None
        assert open_atom is None
        members = {}
        for i in range(n):
            members.setdefault(atom_of[i], []).append(i)
        preds = {a: set() for a in members}
        lw, rd = {}, {}
        for i, o in enumerate(ops):
            a = atom_of[i]
            for r in o["reads"]:
                w = lw.get(id(r))
                if w is not None and w != a:
                    preds[a].add(w)
                if r.x:
                    for x in rd.get(id(r), ()):
                        if x != a and ops[x]["eng"] is not o["eng"]:
                            preds[a].add(x)
            for w_ in o["writes"]:
                w = lw.get(id(w_))
                if w is not None and w != a:
                    preds[a].add(w)
                for x in rd.get(id(w_), ()):
                    if x != a:
                        preds[a].add(x)
            for w_ in o["writes"]:
                lw[id(w_)] = a
                rd[id(w_)] = set()
            for r in o["reads"]:
                rd.setdefault(id(r), set()).add(a)
        order = sorted(members)
        if do_schedule:
            succs = {a: [] for a in members}
            npred = {}
            for a, ps in preds.items():
                npred[a] = len(ps)
                for p in ps:
                    succs[p].append(a)
            dur, lat = {}, {}
            for a, mem in members.items():
                d0 = d1 = 0.0
                for i in mem:
                    x, y = self._dur(ops[i])
                    d0 += x
                    d1 = y if ops[i]["kind"] == 1 else d1 + y
                dur[a], lat[a] = d0, d1
            engs = ["pe", "act", "dve", "pool", "sp"]
            tfree = {e: 0.0 for e in engs}
            waiting = {e: [] for e in engs}
            avail = {e: [] for e in engs}
            ready_t = {}
            finish = {}
            SEM = 200.0
            dma_free = [0.0]
            act_cur = [None]
            CLS = {AF.Exp: "A", AF.Ln: "A", AF.Silu: "B", AF.Tanh: "B"}

            def acls(a):
                o_ = ops[a]
                if o_["kind"] != 0 or o_["name"] != "activation":
                    return None
                return CLS.get(o_["k"].get("func"))

            def push(a):
                t = 0.0
                for p in preds[a]:
                    t = max(t, finish[p] + SEM)
                ready_t[a] = t
                heapq.heappush(waiting[ops[a]["eng"].name], (t, a))
            for a in members:
                if npred[a] == 0:
                    push(a)
            order = []
            self.sched_log = []
            self.sched_preds = preds
            self.sched_finish = finish
            remaining = len(members)
            while remaining:
                best = None
                for e in engs:
                    w, av = waiting[e], avail[e]
                    while w and w[0][0] <= tfree[e]:
                        heapq.heappush(av, heapq.heappop(w)[1])
                    if av:
                        pick = av[0]
                        if e == "act" and acls(pick) not in (None, act_cur[0]):
                            if tfree[e] - ready_t[pick] < 20000.0:
                                alt = [x for x in av if acls(x) in (None, act_cur[0])]
                                if alt:
                                    pick = min(alt)
                        cand = (tfree[e], pick, e, True)
                    elif w:
                        cand = (w[0][0], w[0][1], e, False)
                    else:
                        continue
                    if best is None or cand[:2] < best[:2]:
                        best = cand
                t0, a, e, from_av = best
                if from_av:
                    avail[e].remove(a)
                    heapq.heapify(avail[e])
                else:
                    heapq.heappop(waiting[e])
                if e == "act":
                    c_ = acls(a)
                    if c_ is not None and c_ != act_cur[0]:
                        act_cur[0] = c_
                        t0 += 1300.0
                o = ops[a]
                if o["kind"] == 1:
                    tfree[e] = t0 + dur[a]
                    ds = max(t0, dma_free[0])
                    xfer = lat[a] - 2000.0
                    dma_free[0] = ds + xfer
                    finish[a] = ds + lat[a]
                else:
                    tfree[e] = t0 + dur[a]
                    finish[a] = t0 + lat[a]
                order.append(a)
                self.sched_log.append((a, e, t0, tfree[e], ready_t[a]))
                remaining -= 1
                for s in succs[a]:
                    npred[s] -= 1
                    if npred[s] == 0:
                        push(s)
            self.est_ns = max(finish.values())
        for a in order:
            for i in members[a]:
                o = ops[i]
                if o["kind"] == 0:
                    self._emit_op(o)
                else:
                    self._emit_dma(o)
        self.finish()

    def _wait(self, eng, reads, writes):
        deps = {}

        def need(tok):
            if tok is None:
                return
            key, sem, val = tok
            if key == "pe" and eng.name == "pe":
                return
            if key not in deps or deps[key][1] < val:
                deps[key] = (sem, val)
        for r in reads:
            need(r.w)
            if r.x:
                for key, (sem, val) in r.r.items():
                    if key != eng.name:
                        need((key, sem, val))
        for w in writes:
            need(w.w)
            for key, (sem, val) in w.r.items():
                need((key, sem, val))
        for key, (sem, val) in deps.items():
            if eng.seen.get(key, 0) < val:
                eng.e.wait_ge(sem, val)
                eng.seen[key] = val

    def _mark(self, tok, reads, writes):
        for w in writes:
            w.w = tok
            w.r = {}
        for r in reads:
            key, sem, val = tok
            if key not in r.r or r.r[key][1] < val:
                r.r[key] = (sem, val)

    def _emit_op(self, o):
        eng, reads, writes = o["eng"], o["reads"], o["writes"]
        self._wait(eng, reads, writes)
        ins = getattr(eng.e, o["name"])(*o["a"], **o["k"])
        if o["signal"]:
            eng.count += 1
            ins.then_inc(eng.sem, 1)
            tok = (eng.name, eng.sem, eng.count)
        else:
            tok = (eng.name, eng.sem, eng.count + 1)
        self._mark(tok, reads, writes)

    def _emit_dma(self, o):
        q, eng, reads, writes = o["q"], o["eng"], o["reads"], o["writes"]
        n = self.qn[q]
        slot, prev = n % self.KQ, n // self.KQ
        sem = self.qsems[q][slot]
        key = f"dq_{q}{slot}"
        if prev > 0 and eng.seen.get(key, 0) < 16 * prev:
            eng.e.wait_ge(sem, 16 * prev)
            eng.seen[key] = 16 * prev
        self._wait(eng, reads, writes)
        eng.e.dma_start(out=o["out"], in_=o["in_"]).then_inc(sem, 16)
        tok = (key, sem, 16 * (prev + 1))
        self.qn[q] = n + 1
        self._mark(tok, reads, writes)
        if o["is_output"]:
            self.out_tokens.append(tok)

    def finish(self):
        best = {}
        for key, sem, val in self.out_tokens:
            if key not in best or best[key][1] < val:
                best[key] = (sem, val)
        for key, (sem, val) in best.items():
            self.sp.e.wait_ge(sem, val)


def _groups():
    return [(0, 1)] + [(1 + 4 * i, 4) for i in range(4)]


def build_nc(plan=None, debug=False):
    nc = bass.Bass("TRN2", target_bir_lowering=False)
    dram_in = lambda n, s: nc.dram_tensor(n, list(s), F32, kind="ExternalInput").ap()
    x_d = dram_in("x", (SEQ, D))
    meta_d = dram_in("meta", (16, D))
    wq_d = dram_in("w_q", (D, 1024)); wk2_d = dram_in("w_k2", (D, 512)); wv_d = dram_in("w_v", (D, 256))
    wza_d = dram_in("w_za", (D, 1024)); wzs_d = dram_in("w_zs", (D, 2048)); wxbc_d = dram_in("w_xbc", (D, 3072))
    wdt_d = dram_in("w_dt", (D, 32)); wga_d = dram_in("w_ga", (D, 1024)); wgs_d = dram_in("w_gs", (D, 1024))
    woa_d = dram_in("w_oa", (1024, D)); wos_d = dram_in("w_os", (2048, D)); wo_d = dram_in("w_o", (D, D))
    gpre_d = dram_in("gpre_tab", (128, 8)); gpost_d = dram_in("gpost", (1, D))
    dtb_d = dram_in("dt_bias", (1, 32)); alog_d = dram_in("a_log", (1, 32)); dsk_d = dram_in("d_skip", (1, 32))
    sink_d = dram_in("sink_tab", (128, 8)); gn_d = dram_in("gn_tab", (128, 16)); dtab_d = dram_in("dsk_tab", (128, 16))
    cw_d = dram_in("cw_tab", (128, 24 * 4)); cb_d = dram_in("cb_tab", (128, 24))
    ident_d = dram_in("c_ident", (128, 128)); tri_d = dram_in("c_tri", (128, 128))
    mtab_d = dram_in("c_mtab", (128, 8 * 512))
    out_d = nc.dram_tensor("out", [SEQ, D], F32, kind="ExternalOutput").ap()
    csd = nc.dram_tensor("csd", [NB, 32, 128], F32, kind="Internal").ap()
    wos_s = nc.dram_tensor("wos_s", [2048, D], F32, kind="Internal").ap()
    dbg_d = {}
    if debug:
        for n, s in [("d_uT", (128, 8 * 512)), ("d_yatt", (128, 8 * 512)), ("d_xsT", (128, 16 * 512)),
                     ("d_BT", (128, 4 * 512)), ("d_yn", (128, 16 * 512)), ("d_merged", (128, 8 * 512)),
                     ("d_qT", (128, 8 * 512)), ("d_mA", (128, 8 * 512))]:
            dbg_d[n] = nc.dram_tensor(n, list(s), F32, kind="ExternalOutput").ap()

    with ExitStack() as es:
        tr = T(nc, es)
        PE, ACT, DVE, POOL = tr.pe, tr.act, tr.dve, tr.pool

        def sb(name, shape, dt):
            return Buf(es.enter_context(nc.sbuf_tensor("sb_" + name, list(shape), dt)))

        ident = sb("ident", (128, 128), BF16)
        tri = sb("tri", (128, 128), F32)
        ones = sb("ones", (128, 128), BF16)
        mtab = sb("mtab", (128, 8, 512), BF16)
        gpre = sb("gpre", (128, 8), F32)
        gpost = sb("gpost", (128, D), F32)
        dtb = sb("dtb", (128, 32), F32)
        Abc = sb("Abc", (128, 32), F32)
        Dbc = sb("Dbc", (128, 32), F32)
        DI = sb("DI", (128, 16, 128), BF16)
        dtab = sb("dtab", (128, 16), F32)
        esink = sb("esink", (128, 8), F32)
        gn = sb("gn", (128, 16), F32)
        cw = sb("cw", (128, 24, 4), F32)
        cb = sb("cb", (128, 24), F32)
        identf = sb("identf", (128, 128), F32)

        NTM = 512
        uT = sb("uT", (128, 8, NTM), BF16)
        bufA = sb("bufA", (128, 8, NTM), BF16)
        bufB = sb("bufB", (128, 8, NTM), BF16)
        kT2 = sb("kT2", (128, 4, 128 + NTM), BF16)
        kmT = sb("kmT", (128, 4, 16), BF16)
        Vt = sb("Vt", (128, 5, 256), BF16)
        Vm = sb("Vm", (16, 256), BF16)
        zsT = sb("zsT", (128, 16, NTM), BF16)
        xsT = sb("xsT", (128, 16, NTM), BF16)
        BT = sb("BT", (128, 4, NTM), BF16)
        CT = sb("CT", (128, 4, NTM), BF16)
        NW = 3
        wbufs = [sb(f"wbuf{i}", (128, 8, 512), BF16) for i in range(NW)]
        raw = [sb(f"raw{i}", (128, 3 + NTM), F32) for i in range(2)]
        halo = sb("halo", (128, 24, 3), F32)
        cacc = [sb(f"cacc{i}", (128, NTM), F32) for i in range(2)]
        xin = [sb(f"xin{i}", (128, D), F32) for i in range(2)]
        xres = [sb(f"xres{i}", (128, D), F32) for i in range(2)]
        utm = [sb(f"utm{i}", (128, D), BF16) for i in range(1)]
        stat = [sb(f"stat{i}", (128, 8), F32) for i in range(2)]
        PT = [[sb(f"PT{i}{e}", (128, 512), BF16) for e in range(2)] for i in range(2)]
        PTm = [sb(f"PTm{i}", (16, 512), BF16) for i in range(2)]
        den = [sb(f"den{i}", (128, 256), F32) for i in range(2)]
        atmp = [sb(f"atmp{i}", (128, 256), F32) for i in range(2)]
        mtmp = [sb(f"mtmp{i}", (128, NTM), F32) for i in range(3)]
        dtv = [sb(f"dtv{i}", (128, 32), F32) for i in range(2)]
        dte = [sb(f"dte{i}", (128, 32), F32) for i in range(2)]
        dtt = [sb(f"dtt{i}", (128, 32), F32) for i in range(2)]
        lndt = [sb(f"lndt{i}", (128, 32), F32) for i in range(2)]
        av = [sb(f"av{i}", (128, 32), F32) for i in range(2)]
        cs_sb = [sb(f"cs{i}", (128, 32), F32) for i in range(2)]
        bias1 = [sb(f"bias1{i}", (128, 32), F32) for i in range(4)]
        ecs = [sb(f"ecs{i}", (128, 32), F32) for i in range(4)]
        csT_sb = [sb(f"csT{i}", (32, 128), F32) for i in range(2)]
        csbc = [sb(f"csbc{i}", (128, 8, 128), F32) for i in range(2)]
        darg = [sb(f"darg{i}", (128, 8), F32) for i in range(2)]
        dsdt = [sb(f"dsdt{i}", (128, 8), F32) for i in range(2)]
        cdbc = [sb(f"cdbc{i}", (128, 8), F32) for i in range(2)]
        CBm = [sb(f"CBm{i}", (128, 4, 128), BF16) for i in range(2)]
        Eb = [sb(f"Eb{i}", (128, 8, 128), BF16) for i in range(2)]
        xtm = [sb(f"xtm{i}", (128, 512), BF16) for i in range(2)]
        xrd = [sb(f"xrd{i}", (128, 512), BF16) for i in range(2)]
        Btm = [sb(f"Btm{i}", (128, 128), BF16) for i in range(2)]
        yos = [sb(f"yos{i}", (128, 512), BF16) for i in range(2)]
        Sst = sb("Sst", (128, 2048), F32)
        prevb = sb("prevb", (128, 2048), BF16)
        stmp = [sb(f"stmp{i}", (128, 512), F32) for i in range(2)]
        ysq = [sb(f"ysq{i}", (128, 4, 128), BF16) for i in range(2)]
        nv = [sb(f"nv{i}", (128, 384), F32) for i in range(2)]
        wtmp = xin

        banks = [Buf(es.enter_context(nc.psum_tensor(f"ps{i}", [128, 512], F32)), excl=True) for i in range(8)]
        bank_i = [0]

        def nbank():
            b = banks[bank_i[0] % 8]
            bank_i[0] += 1
            return b

        cnt = {}

        def rot(lst, name):
            i = cnt.get(name, 0)
            cnt[name] = i + 1
            return lst[i % len(lst)]

        A = "all"

        def mm(out, lhsT, rhs, start, stop, reads, writes, signal):
            tr.op(PE, lambda e: e.matmul(out, lhsT=lhsT, rhs=rhs, start=start, stop=stop),
                  reads=reads, writes=writes, signal=signal)

        def load_sp(dst_buf, dst_ap, src_ap, key=A):
            tr.dma("sp", dst_ap, src_ap, writes=dst_buf.r(key))

        def load_cast(dst_buf, dst_ap, src_ap, key=A, extra_reads=()):
            tr.dma("pool", dst_ap, src_ap, reads=list(extra_reads), writes=dst_buf.r(key))

        load_cast(ident, ident[:, :], ident_d)
        load_sp(identf, identf[:, :], ident_d)
        load_sp(tri, tri[:, :], tri_d)
        load_cast(mtab, mtab[:, :, :], mtab_d.rearrange("p (a b) -> p a b", a=8))
        load_sp(gpre, gpre[:, :], gpre_d)
        load_sp(gpost, gpost[:, :], gpost_d.partition_broadcast(128))
        load_sp(dtb, dtb[:, :], dtb_d.partition_broadcast(128))
        load_sp(Abc, Abc[:, :], alog_d.partition_broadcast(128))
        load_sp(Dbc, Dbc[:, :], dsk_d.partition_broadcast(128))
        load_sp(esink, esink[:, :], sink_d)
        load_sp(gn, gn[:, :], gn_d)
        load_sp(cw, cw[:, :, :], cw_d.rearrange("p (c k) -> p c k", k=4))
        load_sp(cb, cb[:, :], cb_d)
        tr.op(POOL, lambda e: e.memset(ones[:, :], 1.0), writes=ones.r(A))
        tr.op(POOL, lambda e: e.memset(halo[:, :, :], 0.0), writes=halo.r(*range(24)))
        tr.op(POOL, lambda e: e.memset(Sst[:, :], 0.0), writes=Sst.r(A))
        tr.op(POOL, lambda e: e.memset(prevb[:, :], 0.0), writes=prevb.r(A))
        tr.op(ACT, lambda e: e.activation(out=Abc[:, :], in_=Abc[:, :], func=AF.Exp), reads=Abc.r(A), writes=Abc.r(A))
        tr.op(DVE, lambda e: e.tensor_scalar(out=Abc[:, :], in0=Abc[:, :], scalar1=-1.0, scalar2=None, op0=ALU.mult),
              reads=Abc.r(A), writes=Abc.r(A))
        tr.op(ACT, lambda e: e.activation(out=esink[:, :], in_=esink[:, :], func=AF.Exp), reads=esink.r(A), writes=esink.r(A))
        load_sp(dtab, dtab[:, :], dtab_d)
        tr.op(DVE, lambda e: e.tensor_tensor(out=DI[:, :, :],
                                             in0=identf[:, :].unsqueeze(1).broadcast_to([128, 16, 128]),
                                             in1=dtab[:, :].unsqueeze(2).broadcast_to([128, 16, 128]), op=ALU.mult),
              reads=identf.r(A) + dtab.r(A), writes=DI.r(A))
        wos_sB = Buf(wos_s)
        def wos_prep():
            for kc in range(16):
                wt = rot(xres, "xres")
                tr.dma("sp", wt[:, :], wos_d[kc * 128:(kc + 1) * 128, :], reads=zsT.r((15, 0)), writes=wt.r(A))
                tr.op(DVE, lambda e: e.tensor_scalar(out=wt[:, :], in0=wt[:, :], scalar1=gn[:, kc:kc + 1], scalar2=None,
                                                      op0=ALU.mult), reads=wt.r(A) + gn.r(A), writes=wt.r(A))
                tr.dma("sp", wos_s[kc * 128:(kc + 1) * 128, :], wt[:, :], reads=wt.r(A), writes=wos_sB.r(kc // 8))

        csdB = Buf(csd)

        wdram = {"wq": wq_d, "wk2": wk2_d, "wza": wza_d, "wzs": wzs_d, "wxbc": wxbc_d, "wga": wga_d, "wgs": wgs_d,
                 "woa": woa_d, "wo": wo_d}
        wrec = []
        wstate = {"i": 0, "issued": 0}
        WDEPTH = 1

        def w_issue(j, key):
            wb = wbufs[j % NW]
            kind = key[0]
            if kind == "wvdt":
                load_cast(wb, wb[:, :, 0:256], wview(wv_d, 256, 0, 256))
                load_cast(wb, wb[:, :, 256:288], wview(wdt_d, 32, 0, 32))
            elif kind == "wos":
                _, kh, mh = key
                load_cast(wb, wb[:, :, :], wos_s[kh * 1024:(kh + 1) * 1024, :].rearrange("(kc p) n -> p kc n", p=128)[:, :, mh * 512:(mh + 1) * 512],
                          extra_reads=wos_sB.r(kh))
            else:
                _, c0, c1 = key
                load_cast(wb, wb[:, :, 0:c1 - c0], wview(wdram[kind], 0, c0, c1))

        def wget(key):
            i = wstate["i"]
            wstate["i"] = i + 1
            wrec.append(key)
            todo = [(i, key)]
            for j, k_ in todo:
                w_issue(j, k_)
                wstate["issued"] = j + 1
            return wbufs[i % NW]


        def wview(wd, ncols_total, c0, c1):
            return wd.rearrange("(kc p) n -> p kc n", p=128)[:, :, c0:c1]

        def interleave(items):
            st = [[g, 0, max(1, n)] for g, n in items]
            while st:
                st.sort(key=lambda s: s[1] / s[2])
                s = st[0]
                try:
                    next(s[0])
                    s[1] += 1
                except StopIteration:
                    st.remove(s)

        def run(gen):
            for _ in gen:
                pass

        def make_group(gi, b0, nb):
            NT = 128 * nb
            first = (gi == 0)
            blks = list(range(nb))
            uT_all = uT.r(*blks)
            u_src = lambda kc: uT[:, kc, 0:NT]
            G = {}

            def proj_fm(wname, ncols, src, evac, src_reads, nk=8):
                for wc in range((ncols + 511) // 512):
                    c0 = wc * 512
                    cw_ = min(512, ncols - c0)
                    wb = wget((wname, c0, c0 + cw_))
                    for m in range(cw_ // 128):
                        pb = nbank()
                        for kc in range(nk):
                            mm(pb[:, 0:NT], wb[:, kc, m * 128:(m + 1) * 128], src(kc), kc == 0, kc == nk - 1,
                               reads=wb.r(A) + src_reads, writes=pb.r(A), signal=(kc == nk - 1))
                        evac(wc * 4 + m, pb)
                        yield

            def s0():
                for bl in blks:
                    b = b0 + bl
                    xi = rot(xin, "xin")
                    st = rot(stat, "stat")
                    ut = rot(utm, "utm")
                    if b == 0:
                        tr.op(DVE, lambda e: e.memset(xi[:, :], 0.0), writes=xi.r(A))
                        tr.dma("sp", xi[112:128, :], meta_d, writes=xi.r(A))
                    else:
                        tr.dma("sp", xi[:, :], x_d[(b - 1) * 128:b * 128, :], writes=xi.r(A))
                    tr.op(DVE, lambda e: e.memset(st[:, 0:1], 0.0), writes=st.r(A))
                    tr.op(ACT, lambda e: e.activation(out=ut[:, :], in_=xi[:, :], func=AF.Square, accum_out=st[:, 0:1]),
                          reads=xi.r(A) + st.r(A), writes=ut.r(A) + st.r(A))
                    tr.op(DVE, lambda e: e.tensor_scalar(out=st[:, 1:2], in0=st[:, 0:1], scalar1=1.0 / D, scalar2=EPS,
                                                         op0=ALU.mult, op1=ALU.add), reads=st.r(A), writes=st.r(A))
                    tr.op(ACT, lambda e: e.activation(out=st[:, 2:3], in_=st[:, 1:2], func=AF.Ln), reads=st.r(A), writes=st.r(A))
                    tr.op(ACT, lambda e: e.activation(out=st[:, 3:4], in_=st[:, 2:3], func=AF.Exp, scale=-0.5),
                          reads=st.r(A), writes=st.r(A))
                    tr.op(DVE, lambda e: e.tensor_scalar(out=ut[:, :], in0=xi[:, :], scalar1=st[:, 3:4], scalar2=None,
                                                         op0=ALU.mult), reads=xi.r(A) + st.r(A), writes=ut.r(A))
                    pb = nbank()
                    pbv = pb[:, :].bitcast(BF16)
                    for kc in range(8):
                        tr.op(PE, lambda e: e.transpose(pbv[:, kc * 128:(kc + 1) * 128], ut[:, kc * 128:(kc + 1) * 128], ident[:, :]),
                              reads=ut.r(A) + ident.r(A), writes=pb.r(A), signal=(kc == 7))
                    tr.op(DVE, lambda e: e.tensor_tensor(out=uT[:, :, bl * 128:(bl + 1) * 128],
                                                         in0=pbv.rearrange("p (c t) -> p c t", c=8),
                                                         in1=gpre[:, :].unsqueeze(2).broadcast_to([128, 8, 128]), op=ALU.mult),
                          reads=pb.r(A) + gpre.r(A), writes=uT.r(bl))
                    yield
            G["s0"] = s0

            def proj1():
                def ev_k(c, pb):
                    tr.op(DVE, lambda e: e.tensor_copy(out=kT2[:, c, 128:128 + NT], in_=pb[:, 0:NT]),
                          reads=pb.r(A), writes=kT2.r(A))
                yield from proj_fm("wk2", 512, u_src, ev_k, uT_all)
                if first:
                    tr.op(DVE, lambda e: e.tensor_copy(out=kmT[:, :, :], in_=kT2[:, :, 128 + 112:128 + 128]),
                          reads=kT2.r(A), writes=kmT.r(A))
                wbv = wget(("wvdt",))
                for bl in blks:
                    pb = nbank()
                    for kc in range(8):
                        mm(pb[:, 0:256], uT[:, kc, bl * 128:(bl + 1) * 128], wbv[:, kc, 0:256], kc == 0, kc == 7,
                           reads=wbv.r(A) + uT.r(bl), writes=pb.r(A), signal=(kc == 7))
                    tr.op(ACT, lambda e: e.activation(out=Vt[:, 1 + bl, :], in_=pb[:, 0:256], func=AF.Copy),
                          reads=pb.r(A), writes=Vt.r(1 + bl))
                for bl in blks:
                    b = b0 + bl
                    k = rot([0, 1], "dtk")
                    pb = nbank()
                    for kc in range(8):
                        mm(pb[:, 0:32], uT[:, kc, bl * 128:(bl + 1) * 128], wbv[:, kc, 256:288], kc == 0, kc == 7,
                           reads=wbv.r(A) + uT.r(bl), writes=pb.r(A), signal=(kc == 7))
                    tr.op(DVE, lambda e: e.tensor_tensor(out=dtv[k][:, :], in0=pb[:, 0:32], in1=dtb[:, :], op=ALU.add),
                          reads=pb.r(A) + dtb.r(A), writes=dtv[k].r(A))
                    tr.op(ACT, lambda e: e.activation(out=dte[k][:, :], in_=dtv[k][:, :], func=AF.Exp),
                          reads=dtv[k].r(A), writes=dte[k].r(A))
                    tr.op(ACT, lambda e: e.activation(out=dtt[k][:, :], in_=dte[k][:, :], func=AF.Ln, bias=1.0),
                          reads=dte[k].r(A), writes=dtt[k].r(A))
                    tr.op(ACT, lambda e: e.activation(out=lndt[k][:, :], in_=dtt[k][:, :], func=AF.Ln),
                          reads=dtt[k].r(A), writes=lndt[k].r(A))
                    tr.op(DVE, lambda e: e.tensor_tensor(out=av[k][:, :], in0=dtt[k][:, :], in1=Abc[:, :], op=ALU.mult),
                          reads=dtt[k].r(A) + Abc.r(A), writes=av[k].r(A))
                    pc = nbank()
                    mm(pc[:, 0:32], tri[:, :], av[k][:, :], True, True, reads=tri.r(A) + av[k].r(A), writes=pc.r(A), signal=False)
                    mm(pc[0:32, 128:256], av[k][:, :], tri[:, :], True, True, reads=tri.r(A) + av[k].r(A), writes=pc.r(A), signal=True)
                    tr.op(DVE, lambda e: e.tensor_copy(out=cs_sb[k][:, :], in_=pc[:, 0:32]), reads=pc.r(A), writes=cs_sb[k].r(A))
                    tr.op(DVE, lambda e: e.tensor_copy(out=csT_sb[k][:, :], in_=pc[0:32, 128:256]), reads=pc.r(A), writes=csT_sb[k].r(A))
                    tr.dma("sp", csd[b], csT_sb[k][:, :], reads=csT_sb[k].r(A), writes=csdB.r(b))
                    tr.op(DVE, lambda e: e.tensor_tensor(out=bias1[bl][:, :], in0=lndt[k][:, :], in1=cs_sb[k][:, :], op=ALU.subtract),
                          reads=lndt[k].r(A) + cs_sb[k].r(A), writes=bias1[bl].r(A))
                    tr.op(ACT, lambda e: e.activation(out=ecs[bl][:, :], in_=cs_sb[k][:, :], func=AF.Exp),
                          reads=cs_sb[k].r(A), writes=ecs[bl].r(A))
                if first:
                    pb = nbank()
                    for kc in range(8):
                        mm(pb[0:16, 0:256], uT[:, kc, 112:128], wbv[:, kc, 0:256], kc == 0, kc == 7,
                           reads=wbv.r(A) + uT.r(0), writes=pb.r(A), signal=(kc == 7))
                    tr.op(ACT, lambda e: e.activation(out=Vm[:, :], in_=pb[0:16, 0:256], func=AF.Copy),
                          reads=pb.r(A), writes=Vm.r(A))
                yield
                if not first:
                    def ev_za(c, pb):
                        tr.op(ACT, lambda e: e.activation(out=bufB[:, c, 0:NT], in_=pb[:, 0:NT], func=AF.Silu),
                              reads=pb.r(A), writes=bufB.r(c))
                    yield from proj_fm("wza", 1024, u_src, ev_za, uT_all)

                    def ev_zs(c, pb):
                        tr.op(ACT, lambda e: e.activation(out=zsT[:, c, 0:NT], in_=pb[:, 0:NT], func=AF.Silu),
                              reads=pb.r(A), writes=zsT.r(*[(c, bl) for bl in blks]))
                    yield from proj_fm("wzs", 2048, u_src, ev_zs, uT_all)

                    def ev_q(c, pb):
                        tr.op(ACT, lambda e: e.activation(out=bufA[:, c, 0:NT], in_=pb[:, 0:NT], func=AF.Copy, scale=0.125),
                              reads=pb.r(A), writes=bufA.r(c))
                    yield from proj_fm("wq", 1024, u_src, ev_q, uT_all)
            G["proj1"] = proj1

            def attn():
                for bl in blks:
                    b = b0 + bl
                    kbs = [1] if b == 1 else [0, 1]
                    tok = slice(bl * 128, (bl + 1) * 128)
                    for g in range(4):
                        stb = [nbank(), nbank()]
                        smb = nbank()
                        pt = rot(PT, "PT")
                        ptm = rot(PTm, "PTm")
                        c0 = 0 if b != 1 else 256
                        for e_ in range(2):
                            ps = slice(e_ * 64, (e_ + 1) * 64)
                            mm(stb[e_][:, c0:512], ident[:, :], mtab[:, 2 * g + e_, c0:512], True, False,
                               reads=ident.r(A) + mtab.r(A), writes=stb[e_].r(A), signal=False)
                            for kb in kbs:
                                kc0 = bl * 128 + kb * 128
                                mm(stb[e_][:, kb * 256:(kb + 1) * 256].rearrange("p (j q) -> p j q", j=2),
                                   kT2[ps, g, kc0:kc0 + 128], bufA[ps, 2 * g:2 * g + 2, tok], False, kb == 1,
                                   reads=kT2.r(A) + bufA.r(2 * g, 2 * g + 1), writes=stb[e_].r(A), signal=(kb == 1))
                            mm(smb[0:16, e_ * 256:(e_ + 1) * 256].rearrange("p (j q) -> p j q", j=2),
                               kmT[ps, g, :], bufA[ps, 2 * g:2 * g + 2, tok], True, True,
                               reads=kmT.r(A) + bufA.r(2 * g, 2 * g + 1), writes=smb.r(A), signal=(e_ == 1))
                        for e_ in range(2):
                            tr.op(ACT, lambda e: e.activation(out=pt[e_][:, c0:512], in_=stb[e_][:, c0:512], func=AF.Exp),
                                  reads=stb[e_].r(A), writes=pt[e_].r(A))
                        tr.op(ACT, lambda e: e.activation(out=ptm[:, :], in_=smb[0:16, :], func=AF.Exp),
                              reads=smb.r(A), writes=ptm.r(A))
                        ob = nbank()
                        for part in range(2):
                            for e_ in range(2):
                                ps = slice(e_ * 64, (e_ + 1) * 64)
                                oc = slice(part * 256, (part + 1) * 256)
                                for i, kb in enumerate(kbs):
                                    slot = bl + kb
                                    lhs = Vt[:, slot, g * 64:(g + 1) * 64] if part == 0 else ones[:, 0:64]
                                    mm(ob[ps, oc], lhs, pt[e_][:, kb * 256:(kb + 1) * 256], i == 0, False,
                                       reads=Vt.r(slot) + ones.r(A) + pt[e_].r(A), writes=ob.r(A), signal=False)
                                lhs = Vm[0:16, g * 64:(g + 1) * 64] if part == 0 else ones[0:16, 0:64]
                                mm(ob[ps, oc], lhs, ptm[0:16, e_ * 256:(e_ + 1) * 256], False, True,
                                   reads=Vm.r(A) + ones.r(A) + ptm.r(A), writes=ob.r(A), signal=(part == 1 and e_ == 1))
                        dn = rot(den, "den")
                        at = rot(atmp, "atmp")
                        tr.op(DVE, lambda e: e.tensor_tensor(out=dn[:, :].rearrange("p (j q) -> p j q", j=2),
                                                             in0=ob[:, 256:512].rearrange("p (j q) -> p j q", j=2),
                                                             in1=esink[:, 2 * g:2 * g + 2].unsqueeze(2).broadcast_to([128, 2, 128]),
                                                             op=ALU.add),
                              reads=ob.r(A) + esink.r(A), writes=dn.r(A))
                        tr.op(DVE, lambda e: e.reciprocal(out=dn[:, :], in_=dn[:, :]), reads=dn.r(A), writes=dn.r(A))
                        tr.op(DVE, lambda e: e.tensor_tensor(out=at[:, :], in0=ob[:, 0:256], in1=dn[:, :], op=ALU.mult),
                              reads=ob.r(A) + dn.r(A), writes=at.r(A))
                        tr.op(DVE, lambda e: e.tensor_tensor(out=bufB[:, 2 * g:2 * g + 2, tok],
                                                             in0=at[:, :].rearrange("p (j q) -> p j q", j=2),
                                                             in1=bufB[:, 2 * g:2 * g + 2, tok], op=ALU.mult),
                              reads=at.r(A) + bufB.r(2 * g, 2 * g + 1), writes=bufB.r(2 * g, 2 * g + 1))
                        yield
                slide()
            G["attn"] = attn

            def slide():
                tr.op(POOL, lambda e: e.tensor_copy(out=kT2[:, :, 0:128], in_=kT2[:, :, NT:NT + 128]),
                      reads=kT2.r(A), writes=kT2.r(A))
                tr.op(POOL, lambda e: e.tensor_copy(out=Vt[:, 0, :], in_=Vt[:, nb, :]), reads=Vt.r(nb), writes=Vt.r(0))
            G["slide"] = slide

            def xbc():
                for wc in range(6):
                    wb = wget(("wxbc", wc * 512, (wc + 1) * 512))
                    for m in range(4):
                        c24 = wc * 4 + m
                        rw = rot(raw, "raw")
                        ca = rot(cacc, "cacc")
                        pb = nbank()
                        for kc in range(8):
                            mm(pb[:, 0:NT], wb[:, kc, m * 128:(m + 1) * 128], uT[:, kc, 0:NT], kc == 0, kc == 7,
                               reads=wb.r(A) + uT_all, writes=pb.r(A), signal=(kc == 7))
                        tr.op(POOL, lambda e: e.tensor_copy(out=rw[:, 0:3], in_=halo[:, c24, :]), reads=halo.r(c24), writes=rw.r(A))
                        tr.op(ACT, lambda e: e.activation(out=rw[:, 3:3 + NT], in_=pb[:, 0:NT], func=AF.Copy),
                              reads=pb.r(A), writes=rw.r(A))
                        tr.op(POOL, lambda e: e.tensor_copy(out=halo[:, c24, :], in_=rw[:, NT:NT + 3]), reads=rw.r(A), writes=halo.r(c24))
                        eng = DVE
                        tr.op(ACT, lambda e: e.activation(out=ca[:, 0:NT], in_=pb[:, 0:NT], func=AF.Identity,
                                                          scale=cw[:, c24, 3:4], bias=cb[:, c24:c24 + 1]),
                              reads=pb.r(A) + cw.r(A) + cb.r(A), writes=ca.r(A))
                        if c24 < 16:
                            dst, dc = xsT, c24
                        elif c24 < 20:
                            dst, dc = BT, c24 - 16
                        else:
                            dst, dc = CT, c24 - 20
                        for kk in (2, 1):
                            tr.op(eng, lambda e: e.scalar_tensor_tensor(out=ca[:, 0:NT], in0=rw[:, kk:kk + NT],
                                                                        scalar=cw[:, c24, kk:kk + 1], in1=ca[:, 0:NT],
                                                                        op0=ALU.mult, op1=ALU.add),
                                  reads=rw.r(A) + cw.r(A) + ca.r(A), writes=ca.r(A))
                        tr.op(eng, lambda e: e.scalar_tensor_tensor(out=dst[:, dc, 0:NT], in0=rw[:, 0:NT],
                                                                    scalar=cw[:, c24, 0:1], in1=ca[:, 0:NT],
                                                                    op0=ALU.mult, op1=ALU.add),
                              reads=rw.r(A) + cw.r(A) + ca.r(A), writes=dst.r(dc))
                        yield
                for dst, ncn in ((xsT, 16), (BT, 4), (CT, 4)):
                    tr.op(ACT, lambda e: e.activation(out=dst[:, :, 0:NT], in_=dst[:, :, 0:NT], func=AF.Silu),
                          reads=dst.r(*range(ncn)), writes=dst.r(*range(ncn)))
                    if first:
                        tr.op(DVE, lambda e: e.memset(dst[:, :, 0:112], 0.0), writes=dst.r(*range(ncn)))
            G["xbc"] = xbc

            def scan():
                for bl in blks:
                    b = b0 + bl
                    k = bl
                    tok = slice(bl * 128, (bl + 1) * 128)
                    pcb = nbank()
                    for g in range(4):
                        mm(pcb[:, g * 128:(g + 1) * 128], BT[:, g, tok], CT[:, g, tok], True, True,
                           reads=BT.r(g) + CT.r(g), writes=pcb.r(A), signal=(g == 3))
                    cbm = rot(CBm, "CBm")
                    tr.op(DVE, lambda e: e.tensor_tensor(out=cbm[:, :, :], in0=pcb[:, :].rearrange("p (g l) -> p g l", g=4),
                                                         in1=tri[:, :].unsqueeze(1).broadcast_to([128, 4, 128]), op=ALU.mult),
                          reads=pcb.r(A) + tri.r(A), writes=cbm.r(A))
                    for g in range(4):
                        hs = slice(g * 8, (g + 1) * 8)
                        xs_r = xsT.r(*range(g * 4, g * 4 + 4))
                        cbc = rot(csbc, "csbc")
                        tr.dma("sp", cbc[:, :, :], csd[b, g * 8:(g + 1) * 8, :].partition_broadcast(128),
                               reads=csdB.r(b), writes=cbc.r(A))
                        da = rot(darg, "darg"); dd = rot(dsdt, "dsdt"); cd = rot(cdbc, "cdbc")
                        tr.op(DVE, lambda e: e.tensor_tensor(out=da[:, :], in0=cbc[:, :, 127], in1=bias1[k][:, hs], op=ALU.add),
                              reads=cbc.r(A) + bias1[k].r(A), writes=da.r(A))
                        tr.op(ACT, lambda e: e.activation(out=dd[:, :], in_=da[:, :], func=AF.Exp), reads=da.r(A), writes=dd.r(A))
                        tr.op(ACT, lambda e: e.activation(out=cd[:, :], in_=cbc[:, :, 127], func=AF.Exp), reads=cbc.r(A), writes=cd.r(A))
                        eb = rot(Eb, "Eb"); lt = eb
                        tr.op(POOL, lambda e: e.tensor_tensor(out=cbc[:, :, :], in0=cbc[:, :, :],
                                                              in1=bias1[k][:, hs].unsqueeze(2).broadcast_to([128, 8, 128]), op=ALU.add),
                              reads=cbc.r(A) + bias1[k].r(A), writes=cbc.r(A))
                        tr.op(ACT, lambda e: e.activation(out=eb[:, :, :], in_=cbc[:, :, :], func=AF.Exp),
                              reads=cbc.r(A), writes=eb.r(A))
                        tr.op(DVE, lambda e: e.scalar_tensor_tensor(out=lt[:, :, :], in0=eb[:, :, :], scalar=BIG,
                                                                    in1=cbm[:, g, :].unsqueeze(1).broadcast_to([128, 8, 128]),
                                                                    op0=ALU.min, op1=ALU.mult),
                              reads=eb.r(A) + cbm.r(A), writes=lt.r(A))
                        ptx = nbank()
                        ptxv = ptx[:, :].bitcast(BF16)
                        for c4 in range(4):
                            tr.op(PE, lambda e: e.transpose(ptxv[:, c4 * 128:(c4 + 1) * 128], xsT[:, g * 4 + c4, tok], ident[:, :]),
                                  reads=xs_r + ident.r(A), writes=ptx.r(A), signal=False)
                        tr.op(PE, lambda e: e.transpose(ptxv[:, 512:640], BT[:, g, tok], ident[:, :]),
                              reads=BT.r(g) + ident.r(A), writes=ptx.r(A), signal=True)
                        xt = rot(xtm, "xtm"); xr = rot(xrd, "xrd"); bt = rot(Btm, "Btm")
                        tr.op(ACT, lambda e: e.activation(out=xt[:, :], in_=ptxv[:, 0:512], func=AF.Copy),
                              reads=ptx.r(A), writes=xt.r(A))
                        tr.op(ACT, lambda e: e.activation(out=bt[:, 0:128], in_=ptxv[:, 512:640], func=AF.Copy),
                              reads=ptx.r(A), writes=bt.r(A))
                        tr.op(DVE, lambda e: e.tensor_tensor(out=xr[:, :].rearrange("p (h d) -> p h d", h=8),
                                                             in0=ptxv[:, 0:512].rearrange("p (h d) -> p h d", h=8),
                                                             in1=dd[:, :].unsqueeze(2).broadcast_to([128, 8, 64]), op=ALU.mult),
                              reads=ptx.r(A) + dd.r(A), writes=xr.r(A))
                        if b > 0:
                            pyo = nbank()
                            mm(pyo[:, :], CT[:, g, tok], prevb[:, g * 512:(g + 1) * 512], True, True,
                               reads=CT.r(g) + prevb.r(g), writes=pyo.r(A), signal=True)
                            yo = rot(yos, "yos")
                            tr.op(DVE, lambda e: e.tensor_tensor(out=yo[:, :].rearrange("p (h d) -> p h d", h=8),
                                                                 in0=pyo[:, :].rearrange("p (h d) -> p h d", h=8),
                                                                 in1=ecs[k][:, hs].unsqueeze(2).broadcast_to([128, 8, 64]), op=ALU.mult),
                                  reads=pyo.r(A) + ecs[k].r(A), writes=yo.r(A))
                            py = nbank()
                            for c4 in range(4):
                                osl = slice(c4 * 128, (c4 + 1) * 128)
                                for e_ in range(2):
                                    r = c4 * 2 + e_
                                    h = g * 8 + r
                                    ps = slice(e_ * 64, (e_ + 1) * 64)
                                    mm(py[ps, osl], xt[:, r * 64:(r + 1) * 64], lt[:, r, :], True, False,
                                       reads=xt.r(A) + lt.r(A), writes=py.r(A), signal=False)
                                mm(py[:, osl], DI[:, g * 4 + c4, :], xsT[:, g * 4 + c4, tok], False, False,
                                   reads=DI.r(A) + xs_r, writes=py.r(A), signal=False)
                                mm(py[:, osl], yo[:, c4 * 128:(c4 + 1) * 128], ident[:, :], False, True,
                                   reads=yo.r(A) + ident.r(A), writes=py.r(A), signal=(c4 == 3))
                            zk = [(g * 4 + c4, bl) for c4 in range(4)]
                            tr.op(DVE, lambda e: e.tensor_tensor(out=zsT[:, g * 4:(g + 1) * 4, tok],
                                                                 in0=py[:, :].rearrange("p (c l) -> p c l", c=4),
                                                                 in1=zsT[:, g * 4:(g + 1) * 4, tok], op=ALU.mult),
                                  reads=py.r(A) + zsT.r(*zk), writes=zsT.r(*zk))
                            yq = rot(ysq, "ysq")
                            tr.op(ACT, lambda e: e.activation(out=yq[:, :, :], in_=zsT[:, g * 4:(g + 1) * 4, tok], func=AF.Square),
                                  reads=zsT.r(*zk), writes=yq.r(A))
                            pn = nbank()
                            for c4 in range(4):
                                mm(pn[:, 0:128], ones[:, :], yq[:, c4, :], c4 == 0, c4 == 3,
                                   reads=ones.r(A) + yq.r(A), writes=pn.r(A), signal=(c4 == 3))
                            nvv = rot(nv, "nv")
                            tr.op(DVE, lambda e: e.tensor_scalar(out=nvv[:, 0:128], in0=pn[:, 0:128], scalar1=1.0 / 512, scalar2=EPS,
                                                                 op0=ALU.mult, op1=ALU.add), reads=pn.r(A), writes=nvv.r(A))
                            tr.op(ACT, lambda e: e.activation(out=nvv[:, 128:256], in_=nvv[:, 0:128], func=AF.Ln),
                                  reads=nvv.r(A), writes=nvv.r(A))
                            tr.op(ACT, lambda e: e.activation(out=nvv[:, 256:384], in_=nvv[:, 128:256], func=AF.Exp, scale=-0.5),
                                  reads=nvv.r(A), writes=nvv.r(A))
                            tr.op(POOL, lambda e: e.tensor_tensor(out=zsT[:, g * 4:(g + 1) * 4, tok],
                                                                 in0=zsT[:, g * 4:(g + 1) * 4, tok],
                                                                 in1=nvv[:, 256:384].unsqueeze(1).broadcast_to([128, 4, 128]), op=ALU.mult),
                                  reads=nvv.r(A) + zsT.r(*zk), writes=zsT.r(*zk))
                        pst = nbank()
                        mm(pst[:, :], bt[:, 0:128], xr[:, :], True, True, reads=bt.r(A) + xr.r(A), writes=pst.r(A), signal=True)
                        sm = rot(stmp, "stmp")
                        gs_ = slice(g * 512, (g + 1) * 512)
                        tr.op(POOL, lambda e: e.tensor_tensor(out=sm[:, :].rearrange("p (h d) -> p h d", h=8),
                                                              in0=Sst[:, gs_].rearrange("p (h d) -> p h d", h=8),
                                                              in1=cd[:, :].unsqueeze(2).broadcast_to([128, 8, 64]), op=ALU.mult),
                              reads=Sst.r(g) + cd.r(A), writes=sm.r(A))
                        tr.op(DVE, lambda e: e.tensor_tensor(out=Sst[:, gs_], in0=pst[:, :], in1=sm[:, :], op=ALU.add),
                              reads=pst.r(A) + sm.r(A), writes=Sst.r(g))
                        tr.op(ACT, lambda e: e.activation(out=prevb[:, gs_], in_=Sst[:, gs_], func=AF.Copy),
                              reads=Sst.r(g), writes=prevb.r(g))
                        yield
            G["scan"] = scan

            def s3():
                def sig_evac(dst, c, pb):
                    mt = rot(mtmp, "mtmp")
                    tr.op(ACT, lambda e: e.activation(out=mt[:, 0:NT], in_=pb[:, 0:NT], func=AF.Exp, scale=-1.0),
                          reads=pb.r(A), writes=mt.r(A))
                    tr.op(ACT, lambda e: e.activation(out=mt[:, 0:NT], in_=mt[:, 0:NT], func=AF.Ln, bias=1.0),
                          reads=mt.r(A), writes=mt.r(A))
                    tr.op(ACT, lambda e: e.activation(out=dst[:, c, 0:NT], in_=mt[:, 0:NT], func=AF.Exp, scale=-1.0),
                          reads=mt.r(A), writes=dst.r(c))

                def ev_ga(c, pb):
                    sig_evac(bufA, c, pb)
                yield from proj_fm("wga", 1024, u_src, ev_ga, uT_all)

                def ev_oa(c, pb):
                    tr.op(DVE, lambda e: e.tensor_tensor(out=bufA[:, c, 0:NT], in0=pb[:, 0:NT], in1=bufA[:, c, 0:NT], op=ALU.mult),
                          reads=pb.r(A) + bufA.r(c), writes=bufA.r(c))
                yield from proj_fm("woa", 1024, lambda kc: bufB[:, kc, 0:NT], ev_oa, bufB.r(*range(8)))

                def ev_gs(c, pb):
                    sig_evac(bufB, c, pb)
                yield from proj_fm("wgs", 1024, u_src, ev_gs, uT_all)
            G["s3"] = s3

            def s5():
                zs_all = zsT.r(*[(c, bl) for c in range(16) for bl in blks])
                for mh in range(2):
                    pbs = [nbank() for _ in range(4)]
                    for kh in range(2):
                        wb = wget(("wos", kh, mh))
                        for m in range(4):
                            for kc in range(8):
                                mm(pbs[m][:, 0:NT], wb[:, kc, m * 128:(m + 1) * 128], zsT[:, kh * 8 + kc, 0:NT],
                                   kh == 0 and kc == 0, kh == 1 and kc == 7,
                                   reads=wb.r(A) + zs_all, writes=pbs[m].r(A), signal=(kh == 1 and kc == 7))
                    for m in range(4):
                        c = mh * 4 + m
                        mt = rot(mtmp, "mtmp")
                        tr.op(DVE, lambda e: e.tensor_tensor(out=mt[:, 0:NT], in0=pbs[m][:, 0:NT], in1=bufB[:, c, 0:NT], op=ALU.mult),
                              reads=pbs[m].r(A) + bufB.r(c), writes=mt.r(A))
                        tr.op(POOL, lambda e: e.tensor_tensor(out=bufA[:, c, 0:NT], in0=mt[:, 0:NT], in1=bufA[:, c, 0:NT], op=ALU.add),
                              reads=mt.r(A) + bufA.r(c), writes=bufA.r(c))
                    yield
            G["s5"] = s5

            def s6():
                wo_b = [wget(("wo", half * 512, (half + 1) * 512)) for half in range(2)]
                for bl in blks:
                    b = b0 + bl
                    xi = rot(xres, "xres")
                    st = rot(stat, "stat")
                    tr.dma("sp", xi[:, :], x_d[(b - 1) * 128:b * 128, :], writes=xi.r(A))
                    pbo = [nbank(), nbank()]
                    for half in range(2):
                        for kc in range(8):
                            mm(pbo[half][:, :], bufA[:, kc, bl * 128:(bl + 1) * 128], wo_b[half][:, kc, :], kc == 0, kc == 7,
                               reads=bufA.r(*range(8)) + wo_b[half].r(A), writes=pbo[half].r(A), signal=(kc == 7))
                    tr.op(DVE, lambda e: e.memset(st[:, 0:2], 0.0), writes=st.r(A))
                    mts = [rot(mtmp, "mtmp"), rot(mtmp, "mtmp")]
                    for half in range(2):
                        tr.op(ACT, lambda e: e.activation(out=mts[half][:, :], in_=pbo[half][:, :], func=AF.Square,
                                                          accum_out=st[:, half:half + 1]),
                              reads=pbo[half].r(A) + st.r(A), writes=mts[half].r(A) + st.r(A))
                    tr.op(DVE, lambda e: e.tensor_tensor(out=st[:, 2:3], in0=st[:, 0:1], in1=st[:, 1:2], op=ALU.add),
                          reads=st.r(A), writes=st.r(A))
                    tr.op(DVE, lambda e: e.tensor_scalar(out=st[:, 3:4], in0=st[:, 2:3], scalar1=1.0 / D, scalar2=EPS,
                                                         op0=ALU.mult, op1=ALU.add), reads=st.r(A), writes=st.r(A))
                    tr.op(ACT, lambda e: e.activation(out=st[:, 4:5], in_=st[:, 3:4], func=AF.Ln), reads=st.r(A), writes=st.r(A))
                    tr.op(ACT, lambda e: e.activation(out=st[:, 5:6], in_=st[:, 4:5], func=AF.Exp, scale=-0.5),
                          reads=st.r(A), writes=st.r(A))
                    for half in range(2):
                        hsl = slice(half * 512, (half + 1) * 512)
                        mt = mts[half]
                        tr.op(DVE, lambda e: e.scalar_tensor_tensor(out=mt[:, :], in0=pbo[half][:, :], scalar=st[:, 5:6],
                                                                    in1=gpost[:, hsl], op0=ALU.mult, op1=ALU.mult),
                              reads=pbo[half].r(A) + st.r(A) + gpost.r(A), writes=mt.r(A))
                        tr.op(POOL, lambda e: e.tensor_tensor(out=xi[:, hsl], in0=xi[:, hsl], in1=mt[:, :], op=ALU.add),
                              reads=xi.r(A) + mt.r(A), writes=xi.r(A))
                    tr.dma("sp", out_d[(b - 1) * 128:b * 128, :], xi[:, :], reads=xi.r(A), is_output=True)
                    yield
            G["s6"] = s6
            return G

        groups = _groups()
        GS = [make_group(gi, b0, nb) for gi, (b0, nb) in enumerate(groups)]
        def chain(*gens):
            for g_ in gens:
                yield from g_

        run(GS[0]["s0"]())
        for gi, (b0, nb) in enumerate(groups):
            G = GS[gi]
            run(G["proj1"]())
            if gi == 0:
                G["slide"]()
                run(G["xbc"]())
                run(G["scan"]())
                run(GS[1]["s0"]())
                continue
            if gi == 1:
                wos_prep()
            interleave([(G["attn"](), 4 * nb), (G["xbc"](), 24)])
            interleave([(G["scan"](), 4 * nb), (G["s3"](), 24)])
            if gi + 1 < len(groups):
                interleave([(chain(G["s5"](), G["s6"]()), 6), (GS[gi + 1]["s0"](), 4)])
            else:
                run(chain(G["s5"](), G["s6"]()))

        tr.flush()
    return nc, wrec


def _consts():
    ident = np.eye(128, dtype=np.float32)
    tri = np.triu(np.ones((128, 128), np.float32))
    j = np.arange(128)[:, None].astype(np.float64)
    q = np.arange(128)[None, :].astype(np.float64)
    mt = np.zeros((128, 8, 2, 2, 128), np.float64)
    for g in range(4):
        for e in range(2):
            for jj in range(2):
                h = 4 * g + 2 * jj + e
                slope = 2.0 ** (-8.0 * (h + 1) / 16)
                rel_prev = q + 128 - j
                rel_cur = q - j
                mt[:, 2 * g + e, 0, jj, :] = np.where(q < j, -slope * rel_prev, -30000.0)
                mt[:, 2 * g + e, 1, jj, :] = np.where(q >= j, -slope * rel_cur, -30000.0)
    return ident, tri, mt.reshape(128, 8 * 512).astype(np.float32)


def _prep_inputs(x, meta_tokens, g_pre, w_in, conv_w, conv_b, dt_bias, a_log, d_skip, attn_sinks,
                 g_ssm_norm, w_out_att, w_out_ssm, w_out, g_post):
    f = lambda a: np.ascontiguousarray(a, dtype=np.float32)
    w = w_in[0]
    sp = np.cumsum([1024, 256, 256, 1024, 2048, 3072, 32, 1024, 1024])
    wq, wk, wv, wza, wzs, wxbc, wdt, wga, wgs = np.split(w, sp[:-1], axis=1)
    wk2 = np.concatenate([np.concatenate([wk[:, g * 64:(g + 1) * 64]] * 2, axis=1) for g in range(4)], axis=1)
    ident, tri, mtab = _consts()
    sink_tab = np.zeros((128, 8), np.float32)
    for c in range(8):
        sink_tab[0:64, c] = attn_sinks[0, 2 * c]
        sink_tab[64:128, c] = attn_sinks[0, 2 * c + 1]
    dsk_tab = np.zeros((128, 16), np.float32)
    for c in range(16):
        dsk_tab[0:64, c] = d_skip[0, 2 * c]
        dsk_tab[64:128, c] = d_skip[0, 2 * c + 1]
    cw_tab = conv_w[0].reshape(4, 24, 128).transpose(2, 1, 0).reshape(128, 96)
    cb_tab = conv_b[0].reshape(24, 128).T
    shared = {
        "meta": f(meta_tokens), "w_q": f(wq), "w_k2": f(wk2), "w_v": f(wv), "w_za": f(wza), "w_zs": f(wzs),
        "w_xbc": f(wxbc), "w_dt": f(wdt), "w_ga": f(wga), "w_gs": f(wgs),
        "w_oa": f(w_out_att[0]), "w_os": f(w_out_ssm[0]), "w_o": f(w_out[0]),
        "gpre_tab": f(g_pre[0].reshape(8, 128).T), "gpost": f(g_post[0].reshape(1, D)),
        "dt_bias": f(dt_bias[0].reshape(1, 32)), "a_log": f(a_log[0].reshape(1, 32)), "d_skip": f(d_skip[0].reshape(1, 32)),
        "sink_tab": f(sink_tab), "gn_tab": f(g_ssm_norm[0].reshape(16, 128).T), "dsk_tab": f(dsk_tab),
        "cw_tab": f(cw_tab), "cb_tab": f(cb_tab),
        "c_ident": ident, "c_tri": tri, "c_mtab": mtab,
    }
    return shared


def kernel(x, meta_tokens, g_pre, w_in, conv_w, conv_b, dt_bias, a_log, d_skip, attn_sinks,
           g_ssm_norm, w_out_att, w_out_ssm, w_out, g_post):
    x = np.asarray(x, dtype=np.float32)
    shared = _prep_inputs(x, meta_tokens, g_pre, w_in, conv_w, conv_b, dt_bias, a_log, d_skip, attn_sinks,
                          g_ssm_norm, w_out_att, w_out_ssm, w_out, g_post)
    nc, _ = build_nc()
    in_maps = [dict(shared, x=np.ascontiguousarray(x[c])) for c in range(NCORES)]
    res = run_bass_kernel_spmd(nc, in_maps, core_ids=list(range(NCORES)))
    return np.stack([np.asarray(res.results[c]["out"], dtype=np.float32) for c in range(NCORES)], axis=0)
```

```python
import numpy as np
from contextlib import ExitStack
import concourse.bass as bass
import concourse.mybir as mybir
from concourse.bass_utils import run_bass_kernel_spmd

F32 = mybir.dt.float32
BF16 = mybir.dt.bfloat16
AF = mybir.ActivationFunctionType
ALU = mybir.AluOpType

D = 1024
SEQ = 2048
NB = 17
EPS = 1e-6
BIG = 3.0e38
NCORES = 8
CFG = dict(xt="dve", bt="dve", q="dve", k="dve", v="dve", raw="act", s5add="dve", s6add="pool", nmul="pool", sm="pool",
           ebias="pool", halo="pool", prevb="act", age=10000.0, sem=300.0, w_xbc=24, w_s3=24, w_attn=16, w_scan=16,
           rot_csbc=2, rot_mtmp=3, rot_raw=2, rot_cacc=2, rot_Eb=2, rot_xtm=2, rot_yos=2, rot_stmp=2, rot_den=2)


class Res:
    __slots__ = ("w", "r", "x")

    def __init__(self, excl=False):
        self.w = None
        self.r = {}
        self.x = excl


class Buf:
    def __init__(self, t, excl=False):
        self.t = t
        self.res = {}
        self.excl = excl

    def r(self, *keys):
        out = []
        for k in keys:
            if k not in self.res:
                self.res[k] = Res(self.excl)
            out.append(self.res[k])
        return out

    def __getitem__(self, k):
        return self.t[k]


class Eng:
    def __init__(self, e, name, sem):
        self.e, self.name, self.sem = e, name, sem
        self.count = 0
        self.seen = {}


class _Proxy:
    def __init__(self):
        self.call = None

    def __getattr__(self, name):
        def f(*a, **k):
            self.call = (name, a, k)
            return None
        return f


def _fsz(ap):
    n = 1
    for d in ap.shape[1:]:
        n *= d
    return n


class T:
    def __init__(self, nc, es):
        self.nc = nc
        self.es = es
        mk = lambda n: es.enter_context(nc.semaphore(n))
        self.pe = Eng(nc.tensor, "pe", mk("s_pe"))
        self.act = Eng(nc.scalar, "act", mk("s_act"))
        self.dve = Eng(nc.vector, "dve", mk("s_dve"))
        self.pool = Eng(nc.gpsimd, "pool", mk("s_pool"))
        self.sp = Eng(nc.sync, "sp", mk("s_sp"))
        self.KQ = 12
        self.qsems = {"sp": [mk(f"dq_sp{i}") for i in range(self.KQ)],
                      "pool": [mk(f"dq_pl{i}") for i in range(self.KQ)]}
        self.qn = {"sp": 0, "pool": 0}
        self.out_tokens = []
        self.ops = []

    def op(self, eng, fn, reads=(), writes=(), signal=True):
        p = _Proxy()
        fn(p)
        name, a, k = p.call
        self.ops.append(dict(kind=0, eng=eng, name=name, a=a, k=k, reads=list(reads), writes=list(writes), signal=signal))

    def dma(self, q, out, in_, reads=(), writes=(), is_output=False):
        eng = self.sp if q == "sp" else self.pool
        self.ops.append(dict(kind=1, eng=eng, q=q, out=out, in_=in_, reads=list(reads), writes=list(writes),
                             is_output=is_output, signal=True))

    def _dur(self, o):
        if o["kind"] == 1:
            ap = o["out"]
            nbytes = ap.shape[0] * _fsz(ap) * 4
            return 100.0, 2000.0 + nbytes / 280.0
        name, a, k = o["name"], o["a"], o["k"]
        en = o["eng"].name
        if en == "pe":
            if name == "transpose":
                n, f32 = 128, False
            else:
                n = _fsz(k["rhs"])
                f32 = (k["lhsT"].dtype == F32)
            d = 56.0 + max(n, 64) * 0.41 * (4 if f32 else 1)
            return d, d
        out = k.get("out", a[0] if a else None)
        n = _fsz(out) if out is not None else 64
        if en == "act":
            d = (n + 224) / 1.2 * 1.25
        else:
            two = name in ("tensor_tensor", "scalar_tensor_tensor")
            f = 1.0 if two else 0.5
            d = (n * f + 64) / 0.96 * 1.45
            if en == "pool":
                d *= 1.95
        return d, d

    def flush(self, do_schedule=True):
        import heapq
        ops = self.ops
        n = len(ops)
        atom_of = list(range(n))
        open_atom = None
        for i, o in enumerate(ops):
            if o["kind"] == 0 and o["eng"].name == "pe":
                if open_atom is None:
                    open_atom = i
                atom_of[i] = open_atom
                if o["signal"]:
                    open_atom = None
        assert open_atom is None
        members = {}
        for i in range(n):
            members.setdefault(atom_of[i], []).append(i)
        preds = {a: set() for a in members}
        lw, rd = {}, {}
        for i, o in enumerate(ops):
            a = atom_of[i]
            for r in o["reads"]:
                w = lw.get(id(r))
                if w is not None and w != a:
                    preds[a].add(w)
                if r.x:
                    for x in rd.get(id(r), ()):
                        if x != a and ops[x]["eng"] is not o["eng"]:
                            preds[a].add(x)
            for w_ in o["writes"]:
                w = lw.get(id(w_))
                if w is not None and w != a:
                    preds[a].add(w)
                for x in rd.get(id(w_), ()):
                    if x != a:
                        preds[a].add(x)
            for w_ in o["writes"]:
                lw[id(w_)] = a
                rd[id(w_)] = set()
            for r in o["reads"]:
                rd.setdefault(id(r), set()).add(a)
        order = sorted(members)
        if do_schedule:
            succs = {a: [] for a in members}
            npred = {}
            for a, ps in preds.items():
                npred[a] = len(ps)
                for p in ps:
                    succs[p].append(a)
            dur, lat = {}, {}
            for a, mem in members.items():
                d0 = d1 = 0.0
                for i in mem:
                    x, y = self._dur(ops[i])
                    d0 += x
                    d1 = y if ops[i]["kind"] == 1 else d1 + y
                dur[a], lat[a] = d0, d1
            engs = ["pe", "act", "dve", "pool", "sp"]
            tfree = {e: 0.0 for e in engs}
            waiting = {e: [] for e in engs}
            avail = {e: [] for e in engs}
            ready_t = {}
            finish = {}
            SEM = CFG["sem"]
            dma_free = [0.0]
            act_cur = [None]
            CLS = {AF.Exp: "A", AF.Ln: "A", AF.Silu: "B", AF.Tanh: "B"}

            def acls(a):
                o_ = ops[a]
                if o_["kind"] != 0 or o_["name"] != "activation":
                    return None
                return CLS.get(o_["k"].get("func"))

            def push(a):
                t = 0.0
                for p in preds[a]:
                    t = max(t, finish[p] + SEM)
                ready_t[a] = t
                heapq.heappush(waiting[ops[a]["eng"].name], (t, a))
            for a in members:
                if npred[a] == 0:
                    push(a)
            order = []
            self.sched_log = []
            self.sched_preds = preds
            self.sched_finish = finish
            remaining = len(members)
            while remaining:
                best = None
                for e in engs:
                    w, av = waiting[e], avail[e]
                    while w and w[0][0] <= tfree[e]:
                        heapq.heappush(av, heapq.heappop(w)[1])
                    if av:
                        pick = av[0]
                        if e == "act" and acls(pick) not in (None, act_cur[0]):
                            if tfree[e] - ready_t[pick] < CFG["age"]:
                                alt = [x for x in av if acls(x) in (None, act_cur[0])]
                                if alt:
                                    pick = min(alt)
                        cand = (tfree[e], pick, e, True)
                    elif w:
                        cand = (w[0][0], w[0][1], e, False)
                    else:
                        continue
                    if best is None or cand[:2] < best[:2]:
                        best = cand
                t0, a, e, from_av = best
                if from_av:
                    avail[e].remove(a)
                    heapq.heapify(avail[e])
                else:
                    heapq.heappop(waiting[e])
                if e == "act":
                    c_ = acls(a)
                    if c_ is not None and c_ != act_cur[0]:
                        act_cur[0] = c_
                        t0 += 1300.0
                o = ops[a]
                if o["kind"] == 1:
                    tfree[e] = t0 + dur[a]
                    ds = max(t0, dma_free[0])
                    xfer = lat[a] - 2000.0
                    dma_free[0] = ds + xfer
                    finish[a] = ds + lat[a]
                else:
                    tfree[e] = t0 + dur[a]
                    finish[a] = t0 + lat[a]
                order.append(a)
                self.sched_log.append((a, e, t0, tfree[e], ready_t[a]))
                remaining -= 1
                for s in succs[a]:
                    npred[s] -= 1
                    if npred[s] == 0:
                        push(s)
            self.est_ns = max(finish.values())
        for a in order:
            for i in members[a]:
                o = ops[i]
                if o["kind"] == 0:
                    self._emit_op(o)
                else:
                    self._emit_dma(o)
        self.finish()

    def _wait(self, eng, reads, writes):
        deps = {}

        def need(tok):
            if tok is None:
                return
            key, sem, val = tok
            if key == "pe" and eng.name == "pe":
                return
            if key not in deps or deps[key][1] < val:
                deps[key] = (sem, val)
        for r in reads:
            need(r.w)
            if r.x:
                for key, (sem, val) in r.r.items():
                    if key != eng.name:
                        need((key, sem, val))
        for w in writes:
            need(w.w)
            for key, (sem, val) in w.r.items():
                need((key, sem, val))
        for key, (sem, val) in deps.items():
            if eng.seen.get(key, 0) < val:
                eng.e.wait_ge(sem, val)
                eng.seen[key] = val

    def _mark(self, tok, reads, writes):
        for w in writes:
            w.w = tok
            w.r = {}
        for r in reads:
            key, sem, val = tok
            if key not in r.r or r.r[key][1] < val:
                r.r[key] = (sem, val)

    def _emit_op(self, o):
        eng, reads, writes = o["eng"], o["reads"], o["writes"]
        self._wait(eng, reads, writes)
        ins = getattr(eng.e, o["name"])(*o["a"], **o["k"])
        if o["signal"]:
            eng.count += 1
            ins.then_inc(eng.sem, 1)
            tok = (eng.name, eng.sem, eng.count)
        else:
            tok = (eng.name, eng.sem, eng.count + 1)
        self._mark(tok, reads, writes)

    def _emit_dma(self, o):
        q, eng, reads, writes = o["q"], o["eng"], o["reads"], o["writes"]
        n = self.qn[q]
        slot, prev = n % self.KQ, n // self.KQ
        sem = self.qsems[q][slot]
        key = f"dq_{q}{slot}"
        if prev > 0 and eng.seen.get(key, 0) < 16 * prev:
            eng.e.wait_ge(sem, 16 * prev)
            eng.seen[key] = 16 * prev
        self._wait(eng, reads, writes)
        eng.e.dma_start(out=o["out"], in_=o["in_"]).then_inc(sem, 16)
        tok = (key, sem, 16 * (prev + 1))
        self.qn[q] = n + 1
        self._mark(tok, reads, writes)
        if o["is_output"]:
            self.out_tokens.append(tok)

    def finish(self):
        best = {}
        for key, sem, val in self.out_tokens:
            if key not in best or best[key][1] < val:
                best[key] = (sem, val)
        for key, (sem, val) in best.items():
            self.sp.e.wait_ge(sem, val)


def _groups():
    return [(0, 1)] + [(1 + 4 * i, 4) for i in range(4)]


def build_nc(plan=None, debug=False):
    nc = bass.Bass("TRN2", target_bir_lowering=False)
    dram_in = lambda n, s: nc.dram_tensor(n, list(s), F32, kind="ExternalInput").ap()
    x_d = dram_in("x", (SEQ, D))
    meta_d = dram_in("meta", (16, D))
    wq_d = dram_in("w_q", (D, 1024)); wk2_d = dram_in("w_k2", (D, 512)); wv_d = dram_in("w_v", (D, 256))
    wza_d = dram_in("w_za", (D, 1024)); wzs_d = dram_in("w_zs", (D, 2048)); wxbc_d = dram_in("w_xbc", (D, 3072))
    wdt_d = dram_in("w_dt", (D, 32)); wga_d = dram_in("w_ga", (D, 1024)); wgs_d = dram_in("w_gs", (D, 1024))
    woa_d = dram_in("w_oa", (1024, D)); wos_d = dram_in("w_os", (2048, D)); wo_d = dram_in("w_o", (D, D))
    gpre_d = dram_in("gpre_tab", (128, 8)); gpost_d = dram_in("gpost", (1, D))
    dtb_d = dram_in("dt_bias", (1, 32)); alog_d = dram_in("a_log", (1, 32)); dsk_d = dram_in("d_skip", (1, 32))
    sink_d = dram_in("sink_tab", (128, 8)); gn_d = dram_in("gn_tab", (128, 16)); dtab_d = dram_in("dsk_tab", (128, 16))
    cw_d = dram_in("cw_tab", (128, 24 * 4)); cb_d = dram_in("cb_tab", (128, 24))
    ident_d = dram_in("c_ident", (128, 128)); tri_d = dram_in("c_tri", (128, 128))
    mtab_d = dram_in("c_mtab", (128, 8 * 512))
    out_d = nc.dram_tensor("out", [SEQ, D], F32, kind="ExternalOutput").ap()
    csd = nc.dram_tensor("csd", [NB, 32, 128], F32, kind="Internal").ap()
    wos_s = nc.dram_tensor("wos_s", [2048, D], F32, kind="Internal").ap()
    dbg_d = {}
    if debug:
        for n, s in [("d_uT", (128, 8 * 512)), ("d_yatt", (128, 8 * 512)), ("d_xsT", (128, 16 * 512)),
                     ("d_BT", (128, 4 * 512)), ("d_yn", (128, 16 * 512)), ("d_merged", (128, 8 * 512)),
                     ("d_qT", (128, 8 * 512)), ("d_mA", (128, 8 * 512))]:
            dbg_d[n] = nc.dram_tensor(n, list(s), F32, kind="ExternalOutput").ap()

    with ExitStack() as es:
        tr = T(nc, es)
        PE, ACT, DVE, POOL = tr.pe, tr.act, tr.dve, tr.pool

        def sb(name, shape, dt):
            return Buf(es.enter_context(nc.sbuf_tensor("sb_" + name, list(shape), dt)))

        ident = sb("ident", (128, 128), BF16)
        tri = sb("tri", (128, 128), F32)
        ones = sb("ones", (128, 128), BF16)
        mtab = sb("mtab", (128, 8, 512), BF16)
        gpre = sb("gpre", (128, 8), F32)
        gpost = sb("gpost", (128, D), F32)
        dtb = sb("dtb", (128, 32), F32)
        Abc = sb("Abc", (128, 32), F32)
        Dbc = sb("Dbc", (128, 32), F32)
        DI = sb("DI", (128, 16, 128), BF16)
        dtab = sb("dtab", (128, 16), F32)
        esink = sb("esink", (128, 8), F32)
        gn = sb("gn", (128, 16), F32)
        cw = sb("cw", (128, 24, 4), F32)
        cb = sb("cb", (128, 24), F32)
        identf = sb("identf", (128, 128), F32)

        NTM = 512
        uT = sb("uT", (128, 8, NTM), BF16)
        bufA = sb("bufA", (128, 8, NTM), BF16)
        bufB = sb("bufB", (128, 8, NTM), BF16)
        kT2 = sb("kT2", (128, 4, 128 + NTM), BF16)
        kmT = sb("kmT", (128, 4, 16), BF16)
        Vt = sb("Vt", (128, 5, 256), BF16)
        Vm = sb("Vm", (16, 256), BF16)
        zsT = sb("zsT", (128, 16, NTM), BF16)
        xsT = sb("xsT", (128, 16, NTM), BF16)
        BT = sb("BT", (128, 4, NTM), BF16)
        CT = sb("CT", (128, 4, NTM), BF16)
        NW = 3
        wbufs = [sb(f"wbuf{i}", (128, 8, 512), BF16) for i in range(NW)]
        raw = [sb(f"raw{i}", (128, 3 + NTM), F32) for i in range(CFG["rot_raw"])]
        halo = sb("halo", (128, 24, 3), F32)
        cacc = [sb(f"cacc{i}", (128, NTM), F32) for i in range(CFG["rot_cacc"])]
        xin = [sb(f"xin{i}", (128, D), F32) for i in range(2)]
        xres = [sb(f"xres{i}", (128, D), F32) for i in range(2)]
        utm = [sb(f"utm{i}", (128, D), BF16) for i in range(1)]
        stat = [sb(f"stat{i}", (128, 8), F32) for i in range(2)]
        PT = [[sb(f"PT{i}{e}", (128, 512), BF16) for e in range(2)] for i in range(2)]
        PTm = [sb(f"PTm{i}", (16, 512), BF16) for i in range(2)]
        den = [sb(f"den{i}", (128, 256), F32) for i in range(CFG["rot_den"])]
        atmp = [sb(f"atmp{i}", (128, 256), F32) for i in range(2)]
        mtmp = [sb(f"mtmp{i}", (128, NTM), F32) for i in range(CFG["rot_mtmp"])]
        dtv = [sb(f"dtv{i}", (128, 32), F32) for i in range(2)]
        dte = [sb(f"dte{i}", (128, 32), F32) for i in range(2)]
        dtt = [sb(f"dtt{i}", (128, 32), F32) for i in range(2)]
        lndt = [sb(f"lndt{i}", (128, 32), F32) for i in range(2)]
        av = [sb(f"av{i}", (128, 32), F32) for i in range(2)]
        cs_sb = [sb(f"cs{i}", (128, 32), F32) for i in range(2)]
        bias1 = [sb(f"bias1{i}", (128, 32), F32) for i in range(4)]
        ecs = [sb(f"ecs{i}", (128, 32), F32) for i in range(4)]
        csT_sb = [sb(f"csT{i}", (32, 128), F32) for i in range(2)]
        csbc = [sb(f"csbc{i}", (128, 8, 128), F32) for i in range(CFG["rot_csbc"])]
        darg = [sb(f"darg{i}", (128, 8), F32) for i in range(2)]
        dsdt = [sb(f"dsdt{i}", (128, 8), F32) for i in range(2)]
        cdbc = [sb(f"cdbc{i}", (128, 8), F32) for i in range(2)]
        CBm = [sb(f"CBm{i}", (128, 4, 128), BF16) for i in range(2)]
        Eb = [sb(f"Eb{i}", (128, 8, 128), BF16) for i in range(CFG["rot_Eb"])]
        xtm = [sb(f"xtm{i}", (128, 512), BF16) for i in range(CFG["rot_xtm"])]
        xrd = [sb(f"xrd{i}", (128, 512), BF16) for i in range(2)]
        Btm = [sb(f"Btm{i}", (128, 128), BF16) for i in range(2)]
        yos = [sb(f"yos{i}", (128, 512), BF16) for i in range(CFG["rot_yos"])]
        Sst = sb("Sst", (128, 2048), F32)
        prevb = sb("prevb", (128, 2048), BF16)
        stmp = [sb(f"stmp{i}", (128, 512), F32) for i in range(CFG["rot_stmp"])]
        ysq = [sb(f"ysq{i}", (128, 4, 128), BF16) for i in range(2)]
        nv = [sb(f"nv{i}", (128, 384), F32) for i in range(2)]
        wtmp = xin

        banks = [Buf(es.enter_context(nc.psum_tensor(f"ps{i}", [128, 512], F32)), excl=True) for i in range(8)]
        bank_i = [0]

        def nbank():
            b = banks[bank_i[0] % 8]
            bank_i[0] += 1
            return b

        cnt = {}

        def rot(lst, name):
            i = cnt.get(name, 0)
            cnt[name] = i + 1
            return lst[i % len(lst)]

        A = "all"
        ENGS = {"act": ACT, "dve": DVE, "pool": POOL}

        def copy_op(which, out, in_, reads, writes, scale=None):
            en = CFG[which]
            if en == "act":
                if scale is None:
                    tr.op(ACT, lambda e: e.activation(out=out, in_=in_, func=AF.Copy), reads=reads, writes=writes)
                else:
                    tr.op(ACT, lambda e: e.activation(out=out, in_=in_, func=AF.Copy, scale=scale), reads=reads, writes=writes)
            else:
                if scale is None:
                    tr.op(ENGS[en], lambda e: e.tensor_copy(out=out, in_=in_), reads=reads, writes=writes)
                else:
                    tr.op(ENGS[en], lambda e: e.tensor_scalar(out=out, in0=in_, scalar1=scale, scalar2=None, op0=ALU.mult),
                          reads=reads, writes=writes)

        def mm(out, lhsT, rhs, start, stop, reads, writes, signal):
            tr.op(PE, lambda e: e.matmul(out, lhsT=lhsT, rhs=rhs, start=start, stop=stop),
                  reads=reads, writes=writes, signal=signal)

        def load_sp(dst_buf, dst_ap, src_ap, key=A):
            tr.dma("sp", dst_ap, src_ap, writes=dst_buf.r(key))

        def load_cast(dst_buf, dst_ap, src_ap, key=A, extra_reads=()):
            tr.dma("pool", dst_ap, src_ap, reads=list(extra_reads), writes=dst_buf.r(key))

        load_cast(ident, ident[:, :], ident_d)
        load_sp(identf, identf[:, :], ident_d)
        load_sp(tri, tri[:, :], tri_d)
        load_cast(mtab, mtab[:, :, :], mtab_d.rearrange("p (a b) -> p a b", a=8))
        load_sp(gpre, gpre[:, :], gpre_d)
        load_sp(gpost, gpost[:, :], gpost_d.partition_broadcast(128))
        load_sp(dtb, dtb[:, :], dtb_d.partition_broadcast(128))
        load_sp(Abc, Abc[:, :], alog_d.partition_broadcast(128))
        load_sp(Dbc, Dbc[:, :], dsk_d.partition_broadcast(128))
        load_sp(esink, esink[:, :], sink_d)
        load_sp(gn, gn[:, :], gn_d)
        load_sp(cw, cw[:, :, :], cw_d.rearrange("p (c k) -> p c k", k=4))
        load_sp(cb, cb[:, :], cb_d)
        tr.op(POOL, lambda e: e.memset(ones[:, :], 1.0), writes=ones.r(A))
        tr.op(POOL, lambda e: e.memset(halo[:, :, :], 0.0), writes=halo.r(*range(24)))
        tr.op(POOL, lambda e: e.memset(Sst[:, :], 0.0), writes=Sst.r(A))
        tr.op(POOL, lambda e: e.memset(prevb[:, :], 0.0), writes=prevb.r(A))
        tr.op(ACT, lambda e: e.activation(out=Abc[:, :], in_=Abc[:, :], func=AF.Exp), reads=Abc.r(A), writes=Abc.r(A))
        tr.op(DVE, lambda e: e.tensor_scalar(out=Abc[:, :], in0=Abc[:, :], scalar1=-1.0, scalar2=None, op0=ALU.mult),
              reads=Abc.r(A), writes=Abc.r(A))
        tr.op(ACT, lambda e: e.activation(out=esink[:, :], in_=esink[:, :], func=AF.Exp), reads=esink.r(A), writes=esink.r(A))
        load_sp(dtab, dtab[:, :], dtab_d)
        tr.op(DVE, lambda e: e.tensor_tensor(out=DI[:, :, :],
                                             in0=identf[:, :].unsqueeze(1).broadcast_to([128, 16, 128]),
                                             in1=dtab[:, :].unsqueeze(2).broadcast_to([128, 16, 128]), op=ALU.mult),
              reads=identf.r(A) + dtab.r(A), writes=DI.r(A))
        wos_sB = Buf(wos_s)
        def wos_prep():
            for kc in range(16):
                wt = rot(xres, "xres")
                tr.dma("sp", wt[:, :], wos_d[kc * 128:(kc + 1) * 128, :], reads=zsT.r((15, 0)), writes=wt.r(A))
                tr.op(DVE, lambda e: e.tensor_scalar(out=wt[:, :], in0=wt[:, :], scalar1=gn[:, kc:kc + 1], scalar2=None,
                                                      op0=ALU.mult), reads=wt.r(A) + gn.r(A), writes=wt.r(A))
                tr.dma("sp", wos_s[kc * 128:(kc + 1) * 128, :], wt[:, :], reads=wt.r(A), writes=wos_sB.r(kc // 8))

        csdB = Buf(csd)

        wdram = {"wq": wq_d, "wk2": wk2_d, "wza": wza_d, "wzs": wzs_d, "wxbc": wxbc_d, "wga": wga_d, "wgs": wgs_d,
                 "woa": woa_d, "wo": wo_d}
        wrec = []
        wstate = {"i": 0, "issued": 0}
        WDEPTH = 1

        def w_issue(j, key):
            wb = wbufs[j % NW]
            kind = key[0]
            if kind == "wvdt":
                load_cast(wb, wb[:, :, 0:256], wview(wv_d, 256, 0, 256))
                load_cast(wb, wb[:, :, 256:288], wview(wdt_d, 32, 0, 32))
            elif kind == "wos":
                _, kh, mh = key
                load_cast(wb, wb[:, :, :], wos_s[kh * 1024:(kh + 1) * 1024, :].rearrange("(kc p) n -> p kc n", p=128)[:, :, mh * 512:(mh + 1) * 512],
                          extra_reads=wos_sB.r(kh))
            else:
                _, c0, c1 = key
                load_cast(wb, wb[:, :, 0:c1 - c0], wview(wdram[kind], 0, c0, c1))

        def wget(key):
            i = wstate["i"]
            wstate["i"] = i + 1
            wrec.append(key)
            todo = [(i, key)]
            for j, k_ in todo:
                w_issue(j, k_)
                wstate["issued"] = j + 1
            return wbufs[i % NW]


        def wview(wd, ncols_total, c0, c1):
            return wd.rearrange("(kc p) n -> p kc n", p=128)[:, :, c0:c1]

        def interleave(items):
            st = [[g, 0, max(1, n)] for g, n in items]
            while st:
                st.sort(key=lambda s: s[1] / s[2])
                s = st[0]
                try:
                    next(s[0])
                    s[1] += 1
                except StopIteration:
                    st.remove(s)

        def run(gen):
            for _ in gen:
                pass

        def make_group(gi, b0, nb):
            NT = 128 * nb
            first = (gi == 0)
            blks = list(range(nb))
            uT_all = uT.r(*blks)
            u_src = lambda kc: uT[:, kc, 0:NT]
            G = {}

            def proj_fm(wname, ncols, src, evac, src_reads, nk=8):
                for wc in range((ncols + 511) // 512):
                    c0 = wc * 512
                    cw_ = min(512, ncols - c0)
                    wb = wget((wname, c0, c0 + cw_))
                    for m in range(cw_ // 128):
                        pb = nbank()
                        for kc in range(nk):
                            mm(pb[:, 0:NT], wb[:, kc, m * 128:(m + 1) * 128], src(kc), kc == 0, kc == nk - 1,
                               reads=wb.r(A) + src_reads, writes=pb.r(A), signal=(kc == nk - 1))
                        evac(wc * 4 + m, pb)
                        yield

            def s0():
                for bl in blks:
                    b = b0 + bl
                    xi = rot(xin, "xin")
                    st = rot(stat, "stat")
                    ut = rot(utm, "utm")
                    if b == 0:
                        tr.op(DVE, lambda e: e.memset(xi[:, :], 0.0), writes=xi.r(A))
                        tr.dma("sp", xi[112:128, :], meta_d, writes=xi.r(A))
                    else:
                        tr.dma("sp", xi[:, :], x_d[(b - 1) * 128:b * 128, :], writes=xi.r(A))
                    tr.op(DVE, lambda e: e.memset(st[:, 0:1], 0.0), writes=st.r(A))
                    tr.op(ACT, lambda e: e.activation(out=ut[:, :], in_=xi[:, :], func=AF.Square, accum_out=st[:, 0:1]),
                          reads=xi.r(A) + st.r(A), writes=ut.r(A) + st.r(A))
                    tr.op(DVE, lambda e: e.tensor_scalar(out=st[:, 1:2], in0=st[:, 0:1], scalar1=1.0 / D, scalar2=EPS,
                                                         op0=ALU.mult, op1=ALU.add), reads=st.r(A), writes=st.r(A))
                    tr.op(ACT, lambda e: e.activation(out=st[:, 2:3], in_=st[:, 1:2], func=AF.Ln), reads=st.r(A), writes=st.r(A))
                    tr.op(ACT, lambda e: e.activation(out=st[:, 3:4], in_=st[:, 2:3], func=AF.Exp, scale=-0.5),
                          reads=st.r(A), writes=st.r(A))
                    tr.op(DVE, lambda e: e.tensor_scalar(out=ut[:, :], in0=xi[:, :], scalar1=st[:, 3:4], scalar2=None,
                                                         op0=ALU.mult), reads=xi.r(A) + st.r(A), writes=ut.r(A))
                    pb = nbank()
                    pbv = pb[:, :].bitcast(BF16)
                    for kc in range(8):
                        tr.op(PE, lambda e: e.transpose(pbv[:, kc * 128:(kc + 1) * 128], ut[:, kc * 128:(kc + 1) * 128], ident[:, :]),
                              reads=ut.r(A) + ident.r(A), writes=pb.r(A), signal=(kc == 7))
                    tr.op(DVE, lambda e: e.tensor_tensor(out=uT[:, :, bl * 128:(bl + 1) * 128],
                                                         in0=pbv.rearrange("p (c t) -> p c t", c=8),
                                                         in1=gpre[:, :].unsqueeze(2).broadcast_to([128, 8, 128]), op=ALU.mult),
                          reads=pb.r(A) + gpre.r(A), writes=uT.r(bl))
                    yield
            G["s0"] = s0

            def proj1():
                def ev_k(c, pb):
                    copy_op("k", kT2[:, c, 128:128 + NT], pb[:, 0:NT], pb.r(A), kT2.r(A))
                yield from proj_fm("wk2", 512, u_src, ev_k, uT_all)
                if first:
                    tr.op(DVE, lambda e: e.tensor_copy(out=kmT[:, :, :], in_=kT2[:, :, 128 + 112:128 + 128]),
                          reads=kT2.r(A), writes=kmT.r(A))
                wbv = wget(("wvdt",))
                for bl in blks:
                    pb = nbank()
                    for kc in range(8):
                        mm(pb[:, 0:256], uT[:, kc, bl * 128:(bl + 1) * 128], wbv[:, kc, 0:256], kc == 0, kc == 7,
                           reads=wbv.r(A) + uT.r(bl), writes=pb.r(A), signal=(kc == 7))
                    copy_op("v", Vt[:, 1 + bl, :], pb[:, 0:256], pb.r(A), Vt.r(1 + bl))
                for bl in blks:
                    b = b0 + bl
                    k = rot([0, 1], "dtk")
                    pb = nbank()
                    for kc in range(8):
                        mm(pb[:, 0:32], uT[:, kc, bl * 128:(bl + 1) * 128], wbv[:, kc, 256:288], kc == 0, kc == 7,
                           reads=wbv.r(A) + uT.r(bl), writes=pb.r(A), signal=(kc == 7))
                    tr.op(DVE, lambda e: e.tensor_tensor(out=dtv[k][:, :], in0=pb[:, 0:32], in1=dtb[:, :], op=ALU.add),
                          reads=pb.r(A) + dtb.r(A), writes=dtv[k].r(A))
                    tr.op(ACT, lambda e: e.activation(out=dte[k][:, :], in_=dtv[k][:, :], func=AF.Exp),
                          reads=dtv[k].r(A), writes=dte[k].r(A))
                    tr.op(ACT, lambda e: e.activation(out=dtt[k][:, :], in_=dte[k][:, :], func=AF.Ln, bias=1.0),
                          reads=dte[k].r(A), writes=dtt[k].r(A))
                    tr.op(ACT, lambda e: e.activation(out=lndt[k][:, :], in_=dtt[k][:, :], func=AF.Ln),
                          reads=dtt[k].r(A), writes=lndt[k].r(A))
                    tr.op(DVE, lambda e: e.tensor_tensor(out=av[k][:, :], in0=dtt[k][:, :], in1=Abc[:, :], op=ALU.mult),
                          reads=dtt[k].r(A) + Abc.r(A), writes=av[k].r(A))
                    pc = nbank()
                    mm(pc[:, 0:32], tri[:, :], av[k][:, :], True, True, reads=tri.r(A) + av[k].r(A), writes=pc.r(A), signal=False)
                    mm(pc[0:32, 128:256], av[k][:, :], tri[:, :], True, True, reads=tri.r(A) + av[k].r(A), writes=pc.r(A), signal=True)
                    tr.op(DVE, lambda e: e.tensor_copy(out=cs_sb[k][:, :], in_=pc[:, 0:32]), reads=pc.r(A), writes=cs_sb[k].r(A))
                    tr.op(DVE, lambda e: e.tensor_copy(out=csT_sb[k][:, :], in_=pc[0:32, 128:256]), reads=pc.r(A), writes=csT_sb[k].r(A))
                    tr.dma("sp", csd[b], csT_sb[k][:, :], reads=csT_sb[k].r(A), writes=csdB.r(b))
                    tr.op(DVE, lambda e: e.tensor_tensor(out=bias1[bl][:, :], in0=lndt[k][:, :], in1=cs_sb[k][:, :], op=ALU.subtract),
                          reads=lndt[k].r(A) + cs_sb[k].r(A), writes=bias1[bl].r(A))
                    tr.op(ACT, lambda e: e.activation(out=ecs[bl][:, :], in_=cs_sb[k][:, :], func=AF.Exp),
                          reads=cs_sb[k].r(A), writes=ecs[bl].r(A))
                if first:
                    pb = nbank()
                    for kc in range(8):
                        mm(pb[0:16, 0:256], uT[:, kc, 112:128], wbv[:, kc, 0:256], kc == 0, kc == 7,
                           reads=wbv.r(A) + uT.r(0), writes=pb.r(A), signal=(kc == 7))
                    tr.op(ACT, lambda e: e.activation(out=Vm[:, :], in_=pb[0:16, 0:256], func=AF.Copy),
                          reads=pb.r(A), writes=Vm.r(A))
                yield
                if not first:
                    def ev_za(c, pb):
                        tr.op(ACT, lambda e: e.activation(out=bufB[:, c, 0:NT], in_=pb[:, 0:NT], func=AF.Silu),
                              reads=pb.r(A), writes=bufB.r(c))
                    yield from proj_fm("wza", 1024, u_src, ev_za, uT_all)

                    def ev_zs(c, pb):
                        tr.op(ACT, lambda e: e.activation(out=zsT[:, c, 0:NT], in_=pb[:, 0:NT], func=AF.Silu),
                              reads=pb.r(A), writes=zsT.r(*[(c, bl) for bl in blks]))
                    yield from proj_fm("wzs", 2048, u_src, ev_zs, uT_all)

                    def ev_q(c, pb):
                        copy_op("q", bufA[:, c, 0:NT], pb[:, 0:NT], pb.r(A), bufA.r(c), scale=0.125)
                    yield from proj_fm("wq", 1024, u_src, ev_q, uT_all)
            G["proj1"] = proj1

            def attn():
                for bl in blks:
                    b = b0 + bl
                    kbs = [1] if b == 1 else [0, 1]
                    tok = slice(bl * 128, (bl + 1) * 128)
                    for g in range(4):
                        stb = [nbank(), nbank()]
                        smb = nbank()
                        pt = rot(PT, "PT")
                        ptm = rot(PTm, "PTm")
                        c0 = 0 if b != 1 else 256
                        for e_ in range(2):
                            ps = slice(e_ * 64, (e_ + 1) * 64)
                            mm(stb[e_][:, c0:512], ident[:, :], mtab[:, 2 * g + e_, c0:512], True, False,
                               reads=ident.r(A) + mtab.r(A), writes=stb[e_].r(A), signal=False)
                            for kb in kbs:
                                kc0 = bl * 128 + kb * 128
                                mm(stb[e_][:, kb * 256:(kb + 1) * 256].rearrange("p (j q) -> p j q", j=2),
                                   kT2[ps, g, kc0:kc0 + 128], bufA[ps, 2 * g:2 * g + 2, tok], False, kb == 1,
                                   reads=kT2.r(A) + bufA.r(2 * g, 2 * g + 1), writes=stb[e_].r(A), signal=(kb == 1))
                            mm(smb[0:16, e_ * 256:(e_ + 1) * 256].rearrange("p (j q) -> p j q", j=2),
                               kmT[ps, g, :], bufA[ps, 2 * g:2 * g + 2, tok], True, True,
                               reads=kmT.r(A) + bufA.r(2 * g, 2 * g + 1), writes=smb.r(A), signal=(e_ == 1))
                        for e_ in range(2):
                            tr.op(ACT, lambda e: e.activation(out=pt[e_][:, c0:512], in_=stb[e_][:, c0:512], func=AF.Exp),
                                  reads=stb[e_].r(A), writes=pt[e_].r(A))
                        tr.op(ACT, lambda e: e.activation(out=ptm[:, :], in_=smb[0:16, :], func=AF.Exp),
                              reads=smb.r(A), writes=ptm.r(A))
                        ob = nbank()
                        for part in range(2):
                            for e_ in range(2):
                                ps = slice(e_ * 64, (e_ + 1) * 64)
                                oc = slice(part * 256, (part + 1) * 256)
                                for i, kb in enumerate(kbs):
                                    slot = bl + kb
                                    lhs = Vt[:, slot, g * 64:(g + 1) * 64] if part == 0 else ones[:, 0:64]
                                    mm(ob[ps, oc], lhs, pt[e_][:, kb * 256:(kb + 1) * 256], i == 0, False,
                                       reads=Vt.r(slot) + ones.r(A) + pt[e_].r(A), writes=ob.r(A), signal=False)
                                lhs = Vm[0:16, g * 64:(g + 1) * 64] if part == 0 else ones[0:16, 0:64]
                                mm(ob[ps, oc], lhs, ptm[0:16, e_ * 256:(e_ + 1) * 256], False, True,
                                   reads=Vm.r(A) + ones.r(A) + ptm.r(A), writes=ob.r(A), signal=(part == 1 and e_ == 1))
                        dn = rot(den, "den")
                        at = rot(atmp, "atmp")
                        tr.op(DVE, lambda e: e.tensor_tensor(out=dn[:, :].rearrange("p (j q) -> p j q", j=2),
                                                             in0=ob[:, 256:512].rearrange("p (j q) -> p j q", j=2),
                                                             in1=esink[:, 2 * g:2 * g + 2].unsqueeze(2).broadcast_to([128, 2, 128]),
                                                             op=ALU.add),
                              reads=ob.r(A) + esink.r(A), writes=dn.r(A))
                        tr.op(DVE, lambda e: e.reciprocal(out=dn[:, :], in_=dn[:, :]), reads=dn.r(A), writes=dn.r(A))
                        tr.op(DVE, lambda e: e.tensor_tensor(out=at[:, :], in0=ob[:, 0:256], in1=dn[:, :], op=ALU.mult),
                              reads=ob.r(A) + dn.r(A), writes=at.r(A))
                        tr.op(DVE, lambda e: e.tensor_tensor(out=bufB[:, 2 * g:2 * g + 2, tok],
                                                             in0=at[:, :].rearrange("p (j q) -> p j q", j=2),
                                                             in1=bufB[:, 2 * g:2 * g + 2, tok], op=ALU.mult),
                              reads=at.r(A) + bufB.r(2 * g, 2 * g + 1), writes=bufB.r(2 * g, 2 * g + 1))
                        yield
                slide()
            G["attn"] = attn

            def slide():
                tr.op(POOL, lambda e: e.tensor_copy(out=kT2[:, :, 0:128], in_=kT2[:, :, NT:NT + 128]),
                      reads=kT2.r(A), writes=kT2.r(A))
                tr.op(POOL, lambda e: e.tensor_copy(out=Vt[:, 0, :], in_=Vt[:, nb, :]), reads=Vt.r(nb), writes=Vt.r(0))
            G["slide"] = slide

            def xbc():
                for wc in range(6):
                    wb = wget(("wxbc", wc * 512, (wc + 1) * 512))
                    for m in range(4):
                        c24 = wc * 4 + m
                        rw = rot(raw, "raw")
                        ca = rot(cacc, "cacc")
                        pb = nbank()
                        for kc in range(8):
                            mm(pb[:, 0:NT], wb[:, kc, m * 128:(m + 1) * 128], uT[:, kc, 0:NT], kc == 0, kc == 7,
                               reads=wb.r(A) + uT_all, writes=pb.r(A), signal=(kc == 7))
                        tr.op(ENGS[CFG["halo"]], lambda e: e.tensor_copy(out=rw[:, 0:3], in_=halo[:, c24, :]), reads=halo.r(c24), writes=rw.r(A))
                        copy_op("raw", rw[:, 3:3 + NT], pb[:, 0:NT], pb.r(A), rw.r(A))
                        tr.op(ENGS[CFG["halo"]], lambda e: e.tensor_copy(out=halo[:, c24, :], in_=rw[:, NT:NT + 3]), reads=rw.r(A), writes=halo.r(c24))
                        eng = DVE
                        tr.op(ACT, lambda e: e.activation(out=ca[:, 0:NT], in_=pb[:, 0:NT], func=AF.Identity,
                                                          scale=cw[:, c24, 3:4], bias=cb[:, c24:c24 + 1]),
                              reads=pb.r(A) + cw.r(A) + cb.r(A), writes=ca.r(A))
                        if c24 < 16:
                            dst, dc = xsT, c24
                        elif c24 < 20:
                            dst, dc = BT, c24 - 16
                        else:
                            dst, dc = CT, c24 - 20
                        for kk in (2, 1):
                            tr.op(eng, lambda e: e.scalar_tensor_tensor(out=ca[:, 0:NT], in0=rw[:, kk:kk + NT],
                                                                        scalar=cw[:, c24, kk:kk + 1], in1=ca[:, 0:NT],
                                                                        op0=ALU.mult, op1=ALU.add),
                                  reads=rw.r(A) + cw.r(A) + ca.r(A), writes=ca.r(A))
                        tr.op(eng, lambda e: e.scalar_tensor_tensor(out=dst[:, dc, 0:NT], in0=rw[:, 0:NT],
                                                                    scalar=cw[:, c24, 0:1], in1=ca[:, 0:NT],
                                                                    op0=ALU.mult, op1=ALU.add),
                              reads=rw.r(A) + cw.r(A) + ca.r(A), writes=dst.r(dc))
                        yield
                for dst, ncn in ((xsT, 16), (BT, 4), (CT, 4)):
                    tr.op(ACT, lambda e: e.activation(out=dst[:, :, 0:NT], in_=dst[:, :, 0:NT], func=AF.Silu),
                          reads=dst.r(*range(ncn)), writes=dst.r(*range(ncn)))
                    if first:
                        tr.op(DVE, lambda e: e.memset(dst[:, :, 0:112], 0.0), writes=dst.r(*range(ncn)))
            G["xbc"] = xbc

            def scan():
                for bl in blks:
                    b = b0 + bl
                    k = bl
                    tok = slice(bl * 128, (bl + 1) * 128)
                    pcb = nbank()
                    for g in range(4):
                        mm(pcb[:, g * 128:(g + 1) * 128], BT[:, g, tok], CT[:, g, tok], True, True,
                           reads=BT.r(g) + CT.r(g), writes=pcb.r(A), signal=(g == 3))
                    cbm = rot(CBm, "CBm")
                    tr.op(DVE, lambda e: e.tensor_tensor(out=cbm[:, :, :], in0=pcb[:, :].rearrange("p (g l) -> p g l", g=4),
                                                         in1=tri[:, :].unsqueeze(1).broadcast_to([128, 4, 128]), op=ALU.mult),
                          reads=pcb.r(A) + tri.r(A), writes=cbm.r(A))
                    for g in range(4):
                        hs = slice(g * 8, (g + 1) * 8)
                        xs_r = xsT.r(*range(g * 4, g * 4 + 4))
                        cbc = rot(csbc, "csbc")
                        tr.dma("sp", cbc[:, :, :], csd[b, g * 8:(g + 1) * 8, :].partition_broadcast(128),
                               reads=csdB.r(b), writes=cbc.r(A))
                        da = rot(darg, "darg"); dd = rot(dsdt, "dsdt"); cd = rot(cdbc, "cdbc")
                        tr.op(DVE, lambda e: e.tensor_tensor(out=da[:, :], in0=cbc[:, :, 127], in1=bias1[k][:, hs], op=ALU.add),
                              reads=cbc.r(A) + bias1[k].r(A), writes=da.r(A))
                        tr.op(ACT, lambda e: e.activation(out=dd[:, :], in_=da[:, :], func=AF.Exp), reads=da.r(A), writes=dd.r(A))
                        tr.op(ACT, lambda e: e.activation(out=cd[:, :], in_=cbc[:, :, 127], func=AF.Exp), reads=cbc.r(A), writes=cd.r(A))
                        eb = rot(Eb, "Eb"); lt = eb
                        tr.op(ENGS[CFG["ebias"]], lambda e: e.tensor_tensor(out=cbc[:, :, :], in0=cbc[:, :, :],
                                                              in1=bias1[k][:, hs].unsqueeze(2).broadcast_to([128, 8, 128]), op=ALU.add),
                              reads=cbc.r(A) + bias1[k].r(A), writes=cbc.r(A))
                        tr.op(ACT, lambda e: e.activation(out=eb[:, :, :], in_=cbc[:, :, :], func=AF.Exp),
                              reads=cbc.r(A), writes=eb.r(A))
                        tr.op(DVE, lambda e: e.scalar_tensor_tensor(out=lt[:, :, :], in0=eb[:, :, :], scalar=BIG,
                                                                    in1=cbm[:, g, :].unsqueeze(1).broadcast_to([128, 8, 128]),
                                                                    op0=ALU.min, op1=ALU.mult),
                              reads=eb.r(A) + cbm.r(A), writes=lt.r(A))
                        ptx = nbank()
                        ptxv = ptx[:, :].bitcast(BF16)
                        for c4 in range(4):
                            tr.op(PE, lambda e: e.transpose(ptxv[:, c4 * 128:(c4 + 1) * 128], xsT[:, g * 4 + c4, tok], ident[:, :]),
                                  reads=xs_r + ident.r(A), writes=ptx.r(A), signal=False)
                        tr.op(PE, lambda e: e.transpose(ptxv[:, 512:640], BT[:, g, tok], ident[:, :]),
                              reads=BT.r(g) + ident.r(A), writes=ptx.r(A), signal=True)
                        xt = rot(xtm, "xtm"); xr = rot(xrd, "xrd"); bt = rot(Btm, "Btm")
                        copy_op("xt", xt[:, :], ptxv[:, 0:512], ptx.r(A), xt.r(A))
                        copy_op("bt", bt[:, 0:128], ptxv[:, 512:640], ptx.r(A), bt.r(A))
                        tr.op(DVE, lambda e: e.tensor_tensor(out=xr[:, :].rearrange("p (h d) -> p h d", h=8),
                                                             in0=ptxv[:, 0:512].rearrange("p (h d) -> p h d", h=8),
                                                             in1=dd[:, :].unsqueeze(2).broadcast_to([128, 8, 64]), op=ALU.mult),
                              reads=ptx.r(A) + dd.r(A), writes=xr.r(A))
                        if b > 0:
                            pyo = nbank()
                            mm(pyo[:, :], CT[:, g, tok], prevb[:, g * 512:(g + 1) * 512], True, True,
                               reads=CT.r(g) + prevb.r(g), writes=pyo.r(A), signal=True)
                            yo = rot(yos, "yos")
                            tr.op(DVE, lambda e: e.tensor_tensor(out=yo[:, :].rearrange("p (h d) -> p h d", h=8),
                                                                 in0=pyo[:, :].rearrange("p (h d) -> p h d", h=8),
                                                                 in1=ecs[k][:, hs].unsqueeze(2).broadcast_to([128, 8, 64]), op=ALU.mult),
                                  reads=pyo.r(A) + ecs[k].r(A), writes=yo.r(A))
                            py = nbank()
                            for c4 in range(4):
                                osl = slice(c4 * 128, (c4 + 1) * 128)
                                for e_ in range(2):
                                    r = c4 * 2 + e_
                                    h = g * 8 + r
                                    ps = slice(e_ * 64, (e_ + 1) * 64)
                                    mm(py[ps, osl], xt[:, r * 64:(r + 1) * 64], lt[:, r, :], True, False,
                                       reads=xt.r(A) + lt.r(A), writes=py.r(A), signal=False)
                                mm(py[:, osl], DI[:, g * 4 + c4, :], xsT[:, g * 4 + c4, tok], False, False,
                                   reads=DI.r(A) + xs_r, writes=py.r(A), signal=False)
                                mm(py[:, osl], yo[:, c4 * 128:(c4 + 1) * 128], ident[:, :], False, True,
                                   reads=yo.r(A) + ident.r(A), writes=py.r(A), signal=(c4 == 3))
                            zk = [(g * 4 + c4, bl) for c4 in range(4)]
                            tr.op(DVE, lambda e: e.tensor_tensor(out=zsT[:, g * 4:(g + 1) * 4, tok],
                                                                 in0=py[:, :].rearrange("p (c l) -> p c l", c=4),
                                                                 in1=zsT[:, g * 4:(g + 1) * 4, tok], op=ALU.mult),
                                  reads=py.r(A) + zsT.r(*zk), writes=zsT.r(*zk))
                            yq = rot(ysq, "ysq")
                            tr.op(ACT, lambda e: e.activation(out=yq[:, :, :], in_=zsT[:, g * 4:(g + 1) * 4, tok], func=AF.Square),
                                  reads=zsT.r(*zk), writes=yq.r(A))
                            pn = nbank()
                            for c4 in range(4):
                                mm(pn[:, 0:128], ones[:, :], yq[:, c4, :], c4 == 0, c4 == 3,
                                   reads=ones.r(A) + yq.r(A), writes=pn.r(A), signal=(c4 == 3))
                            nvv = rot(nv, "nv")
                            tr.op(DVE, lambda e: e.tensor_scalar(out=nvv[:, 0:128], in0=pn[:, 0:128], scalar1=1.0 / 512, scalar2=EPS,
                                                                 op0=ALU.mult, op1=ALU.add), reads=pn.r(A), writes=nvv.r(A))
                            tr.op(ACT, lambda e: e.activation(out=nvv[:, 128:256], in_=nvv[:, 0:128], func=AF.Ln),
                                  reads=nvv.r(A), writes=nvv.r(A))
                            tr.op(ACT, lambda e: e.activation(out=nvv[:, 256:384], in_=nvv[:, 128:256], func=AF.Exp, scale=-0.5),
                                  reads=nvv.r(A), writes=nvv.r(A))
                            tr.op(ENGS[CFG["nmul"]], lambda e: e.tensor_tensor(out=zsT[:, g * 4:(g + 1) * 4, tok],
                                                                 in0=zsT[:, g * 4:(g + 1) * 4, tok],
                                                                 in1=nvv[:, 256:384].unsqueeze(1).broadcast_to([128, 4, 128]), op=ALU.mult),
                                  reads=nvv.r(A) + zsT.r(*zk), writes=zsT.r(*zk))
                        pst = nbank()
                        mm(pst[:, :], bt[:, 0:128], xr[:, :], True, True, reads=bt.r(A) + xr.r(A), writes=pst.r(A), signal=True)
                        sm = rot(stmp, "stmp")
                        gs_ = slice(g * 512, (g + 1) * 512)
                        tr.op(ENGS[CFG["sm"]], lambda e: e.tensor_tensor(out=sm[:, :].rearrange("p (h d) -> p h d", h=8),
                                                              in0=Sst[:, gs_].rearrange("p (h d) -> p h d", h=8),
                                                              in1=cd[:, :].unsqueeze(2).broadcast_to([128, 8, 64]), op=ALU.mult),
                              reads=Sst.r(g) + cd.r(A), writes=sm.r(A))
                        tr.op(DVE, lambda e: e.tensor_tensor(out=Sst[:, gs_], in0=pst[:, :], in1=sm[:, :], op=ALU.add),
                              reads=pst.r(A) + sm.r(A), writes=Sst.r(g))
                        copy_op("prevb", prevb[:, gs_], Sst[:, gs_], Sst.r(g), prevb.r(g))
                        yield
            G["scan"] = scan

            def s3():
                def sig_evac(dst, c, pb):
                    mt = rot(mtmp, "mtmp")
                    tr.op(ACT, lambda e: e.activation(out=mt[:, 0:NT], in_=pb[:, 0:NT], func=AF.Exp, scale=-1.0),
                          reads=pb.r(A), writes=mt.r(A))
                    tr.op(ACT, lambda e: e.activation(out=mt[:, 0:NT], in_=mt[:, 0:NT], func=AF.Ln, bias=1.0),
                          reads=mt.r(A), writes=mt.r(A))
                    tr.op(ACT, lambda e: e.activation(out=dst[:, c, 0:NT], in_=mt[:, 0:NT], func=AF.Exp, scale=-1.0),
                          reads=mt.r(A), writes=dst.r(c))

                def ev_ga(c, pb):
                    sig_evac(bufA, c, pb)
                yield from proj_fm("wga", 1024, u_src, ev_ga, uT_all)

                def ev_oa(c, pb):
                    tr.op(DVE, lambda e: e.tensor_tensor(out=bufA[:, c, 0:NT], in0=pb[:, 0:NT], in1=bufA[:, c, 0:NT], op=ALU.mult),
                          reads=pb.r(A) + bufA.r(c), writes=bufA.r(c))
                yield from proj_fm("woa", 1024, lambda kc: bufB[:, kc, 0:NT], ev_oa, bufB.r(*range(8)))

                def ev_gs(c, pb):
                    sig_evac(bufB, c, pb)
                yield from proj_fm("wgs", 1024, u_src, ev_gs, uT_all)
            G["s3"] = s3

            def s5():
                zs_all = zsT.r(*[(c, bl) for c in range(16) for bl in blks])
                for mh in range(2):
                    pbs = [nbank() for _ in range(4)]
                    for kh in range(2):
                        wb = wget(("wos", kh, mh))
                        for m in range(4):
                            for kc in range(8):
                                mm(pbs[m][:, 0:NT], wb[:, kc, m * 128:(m + 1) * 128], zsT[:, kh * 8 + kc, 0:NT],
                                   kh == 0 and kc == 0, kh == 1 and kc == 7,
                                   reads=wb.r(A) + zs_all, writes=pbs[m].r(A), signal=(kh == 1 and kc == 7))
                    for m in range(4):
                        c = mh * 4 + m
                        mt = rot(mtmp, "mtmp")
                        tr.op(DVE, lambda e: e.tensor_tensor(out=mt[:, 0:NT], in0=pbs[m][:, 0:NT], in1=bufB[:, c, 0:NT], op=ALU.mult),
                              reads=pbs[m].r(A) + bufB.r(c), writes=mt.r(A))
                        tr.op(ENGS[CFG["s5add"]], lambda e: e.tensor_tensor(out=bufA[:, c, 0:NT], in0=mt[:, 0:NT], in1=bufA[:, c, 0:NT], op=ALU.add),
                              reads=mt.r(A) + bufA.r(c), writes=bufA.r(c))
                    yield
            G["s5"] = s5

            def s6():
                wo_b = [wget(("wo", half * 512, (half + 1) * 512)) for half in range(2)]
                for bl in blks:
                    b = b0 + bl
                    xi = rot(xres, "xres")
                    st = rot(stat, "stat")
                    tr.dma("sp", xi[:, :], x_d[(b - 1) * 128:b * 128, :], writes=xi.r(A))
                    pbo = [nbank(), nbank()]
                    for half in range(2):
                        for kc in range(8):
                            mm(pbo[half][:, :], bufA[:, kc, bl * 128:(bl + 1) * 128], wo_b[half][:, kc, :], kc == 0, kc == 7,
                               reads=bufA.r(*range(8)) + wo_b[half].r(A), writes=pbo[half].r(A), signal=(kc == 7))
                    tr.op(DVE, lambda e: e.memset(st[:, 0:2], 0.0), writes=st.r(A))
                    mts = [rot(mtmp, "mtmp"), rot(mtmp, "mtmp")]
                    for half in range(2):
                        tr.op(ACT, lambda e: e.activation(out=mts[half][:, :], in_=pbo[half][:, :], func=AF.Square,
                                                          accum_out=st[:, half:half + 1]),
                              reads=pbo[half].r(A) + st.r(A), writes=mts[half].r(A) + st.r(A))
                    tr.op(DVE, lambda e: e.tensor_tensor(out=st[:, 2:3], in0=st[:, 0:1], in1=st[:, 1:2], op=ALU.add),
                          reads=st.r(A), writes=st.r(A))
                    tr.op(DVE, lambda e: e.tensor_scalar(out=st[:, 3:4], in0=st[:, 2:3], scalar1=1.0 / D, scalar2=EPS,
                                                         op0=ALU.mult, op1=ALU.add), reads=st.r(A), writes=st.r(A))
                    tr.op(ACT, lambda e: e.activation(out=st[:, 4:5], in_=st[:, 3:4], func=AF.Ln), reads=st.r(A), writes=st.r(A))
                    tr.op(ACT, lambda e: e.activation(out=st[:, 5:6], in_=st[:, 4:5], func=AF.Exp, scale=-0.5),
                          reads=st.r(A), writes=st.r(A))
                    for half in range(2):
                        hsl = slice(half * 512, (half + 1) * 512)
                        mt = mts[half]
                        tr.op(DVE, lambda e: e.scalar_tensor_tensor(out=mt[:, :], in0=pbo[half][:, :], scalar=st[:, 5:6],
                                                                    in1=gpost[:, hsl], op0=ALU.mult, op1=ALU.mult),
                              reads=pbo[half].r(A) + st.r(A) + gpost.r(A), writes=mt.r(A))
                        tr.op(ENGS[CFG["s6add"]], lambda e: e.tensor_tensor(out=xi[:, hsl], in0=xi[:, hsl], in1=mt[:, :], op=ALU.add),
                              reads=xi.r(A) + mt.r(A), writes=xi.r(A))
                    tr.dma("sp", out_d[(b - 1) * 128:b * 128, :], xi[:, :], reads=xi.r(A), is_output=True)
                    yield
            G["s6"] = s6
            return G

        groups = _groups()
        GS = [make_group(gi, b0, nb) for gi, (b0, nb) in enumerate(groups)]
        def chain(*gens):
            for g_ in gens:
                yield from g_

        run(GS[0]["s0"]())
        for gi, (b0, nb) in enumerate(groups):
            G = GS[gi]
            run(G["proj1"]())
            if gi == 0:
                G["slide"]()
                run(G["xbc"]())
                run(G["scan"]())
                run(GS[1]["s0"]())
                continue
            if gi == 1:
                wos_prep()
            interleave([(G["attn"](), CFG["w_attn"]), (G["xbc"](), CFG["w_xbc"])])
            interleave([(G["scan"](), CFG["w_scan"]), (G["s3"](), CFG["w_s3"])])
            if gi + 1 < len(groups):
                interleave([(chain(G["s5"](), G["s6"]()), 6), (GS[gi + 1]["s0"](), 4)])
            else:
                run(chain(G["s5"](), G["s6"]()))

        tr.flush()
    return nc, wrec


def _consts():
    ident = np.eye(128, dtype=np.float32)
    tri = np.triu(np.ones((128, 128), np.float32))
    j = np.arange(128)[:, None].astype(np.float64)
    q = np.arange(128)[None, :].astype(np.float64)
    mt = np.zeros((128, 8, 2, 2, 128), np.float64)
    for g in range(4):
        for e in range(2):
            for jj in range(2):
                h = 4 * g + 2 * jj + e
                slope = 2.0 ** (-8.0 * (h + 1) / 16)
                rel_prev = q + 128 - j
                rel_cur = q - j
                mt[:, 2 * g + e, 0, jj, :] = np.where(q < j, -slope * rel_prev, -30000.0)
                mt[:, 2 * g + e, 1, jj, :] = np.where(q >= j, -slope * rel_cur, -30000.0)
    return ident, tri, mt.reshape(128, 8 * 512).astype(np.float32)


def _prep_inputs(x, meta_tokens, g_pre, w_in, conv_w, conv_b, dt_bias, a_log, d_skip, attn_sinks,
                 g_ssm_norm, w_out_att, w_out_ssm, w_out, g_post):
    f = lambda a: np.ascontiguousarray(a, dtype=np.float32)
    w = w_in[0]
    sp = np.cumsum([1024, 256, 256, 1024, 2048, 3072, 32, 1024, 1024])
    wq, wk, wv, wza, wzs, wxbc, wdt, wga, wgs = np.split(w, sp[:-1], axis=1)
    wk2 = np.concatenate([np.concatenate([wk[:, g * 64:(g + 1) * 64]] * 2, axis=1) for g in range(4)], axis=1)
    ident, tri, mtab = _consts()
    sink_tab = np.zeros((128, 8), np.float32)
    for c in range(8):
        sink_tab[0:64, c] = attn_sinks[0, 2 * c]
        sink_tab[64:128, c] = attn_sinks[0, 2 * c + 1]
    dsk_tab = np.zeros((128, 16), np.float32)
    for c in range(16):
        dsk_tab[0:64, c] = d_skip[0, 2 * c]
        dsk_tab[64:128, c] = d_skip[0, 2 * c + 1]
    cw_tab = conv_w[0].reshape(4, 24, 128).transpose(2, 1, 0).reshape(128, 96)
    cb_tab = conv_b[0].reshape(24, 128).T
    shared = {
        "meta": f(meta_tokens), "w_q": f(wq), "w_k2": f(wk2), "w_v": f(wv), "w_za": f(wza), "w_zs": f(wzs),
        "w_xbc": f(wxbc), "w_dt": f(wdt), "w_ga": f(wga), "w_gs": f(wgs),
        "w_oa": f(w_out_att[0]), "w_os": f(w_out_ssm[0]), "w_o": f(w_out[0]),
        "gpre_tab": f(g_pre[0].reshape(8, 128).T), "gpost": f(g_post[0].reshape(1, D)),
        "dt_bias": f(dt_bias[0].reshape(1, 32)), "a_log": f(a_log[0].reshape(1, 32)), "d_skip": f(d_skip[0].reshape(1, 32)),
        "sink_tab": f(sink_tab), "gn_tab": f(g_ssm_norm[0].reshape(16, 128).T), "dsk_tab": f(dsk_tab),
        "cw_tab": f(cw_tab), "cb_tab": f(cb_tab),
        "c_ident": ident, "c_tri": tri, "c_mtab": mtab,
    }
    return shared


def kernel(x, meta_tokens, g_pre, w_in, conv_w, conv_b, dt_bias, a_log, d_skip, attn_sinks,
           g_ssm_norm, w_out_att, w_out_ssm, w_out, g_post):
    x = np.asarray(x, dtype=np.float32)
    shared = _prep_inputs(x, meta_tokens, g_pre, w_in, conv_w, conv_b, dt_bias, a_log, d_skip, attn_sinks,
                          g_ssm_norm, w_out_att, w_out_ssm, w_out, g_post)
    nc, _ = build_nc()
    in_maps = [dict(shared, x=np.ascontiguousarray(x[c])) for c in range(NCORES)]
    res = run_bass_kernel_spmd(nc, in_maps, core_ids=list(range(NCORES)))
    return np.stack([np.asarray(res.results[c]["out"], dtype=np.float32) for c in range(NCORES)], axis=0)
```

```python
import numpy as np
from contextlib import ExitStack
import concourse.bass as bass
import concourse.mybir as mybir
from concourse.bass_utils import run_bass_kernel_spmd

F32 = mybir.dt.float32
BF16 = mybir.dt.bfloat16
AF = mybir.ActivationFunctionType
ALU = mybir.AluOpType

D = 1024
SEQ = 2048
NB = 17
EPS = 1e-6
BIG = 3.0e38
NCORES = 8
CFG = dict(xt="dve", bt="dve", q="act", k="dve", v="dve", raw="act", s5add="dve", s6add="pool", nmul="pool", sm="pool",
           ebias="pool", halo="pool", prevb="act", age=10000.0, sem=300.0, w_xbc=24, w_s3=24, w_attn=16, w_scan=16,
           rot_csbc=2, rot_mtmp=3, rot_raw=2, rot_cacc=2, rot_Eb=2, rot_xtm=2, rot_yos=2, rot_stmp=2, rot_den=2)


class Res:
    __slots__ = ("w", "r", "x")

    def __init__(self, excl=False):
        self.w = None
        self.r = {}
        self.x = excl


class Buf:
    def __init__(self, t, excl=False):
        self.t = t
        self.res = {}
        self.excl = excl

    def r(self, *keys):
        out = []
        for k in keys:
            if k not in self.res:
                self.res[k] = Res(self.excl)
            out.append(self.res[k])
        return out

    def __getitem__(self, k):
        return self.t[k]


class Eng:
    def __init__(self, e, name, sem):
        self.e, self.name, self.sem = e, name, sem
        self.count = 0
        self.seen = {}


class _Proxy:
    def __init__(self):
        self.call = None

    def __getattr__(self, name):
        def f(*a, **k):
            self.call = (name, a, k)
            return None
        return f


def _fsz(ap):
    n = 1
    for d in ap.shape[1:]:
        n *= d
    return n


class T:
    def __init__(self, nc, es):
        self.nc = nc
        self.es = es
        mk = lambda n: es.enter_context(nc.semaphore(n))
        self.pe = Eng(nc.tensor, "pe", mk("s_pe"))
        self.act = Eng(nc.scalar, "act", mk("s_act"))
        self.dve = Eng(nc.vector, "dve", mk("s_dve"))
        self.pool = Eng(nc.gpsimd, "pool", mk("s_pool"))
        self.sp = Eng(nc.sync, "sp", mk("s_sp"))
        self.KQ = 12
        self.qsems = {"sp": [mk(f"dq_sp{i}") for i in range(self.KQ)],
                      "pool": [mk(f"dq_pl{i}") for i in range(self.KQ)]}
        self.qn = {"sp": 0, "pool": 0}
        self.out_tokens = []
        self.ops = []

    def op(self, eng, fn, reads=(), writes=(), signal=True):
        p = _Proxy()
        fn(p)
        name, a, k = p.call
        self.ops.append(dict(kind=0, eng=eng, name=name, a=a, k=k, reads=list(reads), writes=list(writes), signal=signal))

    def dma(self, q, out, in_, reads=(), writes=(), is_output=False):
        eng = self.sp if q == "sp" else self.pool
        self.ops.append(dict(kind=1, eng=eng, q=q, out=out, in_=in_, reads=list(reads), writes=list(writes),
                             is_output=is_output, signal=True))

    def _dur(self, o):
        if o["kind"] == 1:
            ap = o["out"]
            nbytes = ap.shape[0] * _fsz(ap) * 4
            return 100.0, 2000.0 + nbytes / 280.0
        name, a, k = o["name"], o["a"], o["k"]
        en = o["eng"].name
        if en == "pe":
            if name == "transpose":
                n, f32 = 128, False
            else:
                n = _fsz(k["rhs"])
                f32 = (k["lhsT"].dtype == F32)
            d = 56.0 + max(n, 64) * 0.41 * (4 if f32 else 1)
            return d, d
        out = k.get("out", a[0] if a else None)
        n = _fsz(out) if out is not None else 64
        if en == "act":
            d = (n + 224) / 1.2 * 1.25
        else:
            two = name in ("tensor_tensor", "scalar_tensor_tensor")
            f = 1.0 if two else 0.5
            d = (n * f + 64) / 0.96 * 1.45
            if en == "pool":
                d *= 1.95
        return d, d

    def flush(self, do_schedule=True):
        import heapq
        ops = self.ops
        n = len(ops)
        atom_of = list(range(n))
        open_atom = None
        for i, o in enumerate(ops):
            if o["kind"] == 0 and o["eng"].name == "pe":
                if open_atom is None:
                    open_atom = i
                atom_of[i] = open_atom
                if o["signal"]:
                    open_atom = None
        assert open_atom is None
        members = {}
        for i in range(n):
            members.setdefault(atom_of[i], []).append(i)
        preds = {a: set() for a in members}
        lw, rd = {}, {}
        for i, o in enumerate(ops):
            a = atom_of[i]
            for r in o["reads"]:
                w = lw.get(id(r))
                if w is not None and w != a:
                    preds[a].add(w)
                if r.x:
                    for x in rd.get(id(r), ()):
                        if x != a and ops[x]["eng"] is not o["eng"]:
                            preds[a].add(x)
            for w_ in o["writes"]:
                w = lw.get(id(w_))
                if w is not None and w != a:
                    preds[a].add(w)
                for x in rd.get(id(w_), ()):
                    if x != a:
                        preds[a].add(x)
            for w_ in o["writes"]:
                lw[id(w_)] = a
                rd[id(w_)] = set()
            for r in o["reads"]:
                rd.setdefault(id(r), set()).add(a)
        order = sorted(members)
        if do_schedule:
            succs = {a: [] for a in members}
            npred = {}
            for a, ps in preds.items():
                npred[a] = len(ps)
                for p in ps:
                    succs[p].append(a)
            dur, lat = {}, {}
            for a, mem in members.items():
                d0 = d1 = 0.0
                for i in mem:
                    x, y = self._dur(ops[i])
                    d0 += x
                    d1 = y if ops[i]["kind"] == 1 else d1 + y
                dur[a], lat[a] = d0, d1
            engs = ["pe", "act", "dve", "pool", "sp"]
            tfree = {e: 0.0 for e in engs}
            waiting = {e: [] for e in engs}
            avail = {e: [] for e in engs}
            ready_t = {}
            finish = {}
            SEM = CFG["sem"]
            dma_free = [0.0]
            act_cur = [None]
            CLS = {AF.Exp: "A", AF.Ln: "A", AF.Silu: "B", AF.Tanh: "B"}

            def acls(a):
                o_ = ops[a]
                if o_["kind"] != 0 or o_["name"] != "activation":
                    return None
                return CLS.get(o_["k"].get("func"))

            def push(a):
                t = 0.0
                for p in preds[a]:
                    t = max(t, finish[p] + SEM)
                ready_t[a] = t
                heapq.heappush(waiting[ops[a]["eng"].name], (t, a))
            for a in members:
                if npred[a] == 0:
                    push(a)
            order = []
            self.sched_log = []
            self.sched_preds = preds
            self.sched_finish = finish
            remaining = len(members)
            while remaining:
                best = None
                for e in engs:
                    w, av = waiting[e], avail[e]
                    while w and w[0][0] <= tfree[e]:
                        heapq.heappush(av, heapq.heappop(w)[1])
                    if av:
                        pick = av[0]
                        if e == "act" and acls(pick) not in (None, act_cur[0]):
                            if tfree[e] - ready_t[pick] < CFG["age"]:
                                alt = [x for x in av if acls(x) in (None, act_cur[0])]
                                if alt:
                                    pick = min(alt)
                        cand = (tfree[e], pick, e, True)
                    elif w:
                        cand = (w[0][0], w[0][1], e, False)
                    else:
                        continue
                    if best is None or cand[:2] < best[:2]:
                        best = cand
                t0, a, e, from_av = best
                if from_av:
                    avail[e].remove(a)
                    heapq.heapify(avail[e])
                else:
                    heapq.heappop(waiting[e])
                if e == "act":
                    c_ = acls(a)
                    if c_ is not None and c_ != act_cur[0]:
                        act_cur[0] = c_
                        t0 += 1300.0
                o = ops[a]
                if o["kind"] == 1:
                    tfree[e] = t0 + dur[a]
                    ds = max(t0, dma_free[0])
                    xfer = lat[a] - 2000.0
                    dma_free[0] = ds + xfer
                    finish[a] = ds + lat[a]
                else:
                    tfree[e] = t0 + dur[a]
                    finish[a] = t0 + lat[a]
                order.append(a)
                self.sched_log.append((a, e, t0, tfree[e], ready_t[a]))
                remaining -= 1
                for s in succs[a]:
                    npred[s] -= 1
                    if npred[s] == 0:
                        push(s)
            self.est_ns = max(finish.values())
        for a in order:
            for i in members[a]:
                o = ops[i]
                if o["kind"] == 0:
                    self._emit_op(o)
                else:
                    self._emit_dma(o)
        self.finish()

    def _wait(self, eng, reads, writes):
        deps = {}

        def need(tok):
            if tok is None:
                return
            key, sem, val = tok
            if key == "pe" and eng.name == "pe":
                return
            if key not in deps or deps[key][1] < val:
                deps[key] = (sem, val)
        for r in reads:
            need(r.w)
            if r.x:
                for key, (sem, val) in r.r.items():
                    if key != eng.name:
                        need((key, sem, val))
        for w in writes:
            need(w.w)
            for key, (sem, val) in w.r.items():
                need((key, sem, val))
        for key, (sem, val) in deps.items():
            if eng.seen.get(key, 0) < val:
                eng.e.wait_ge(sem, val)
                eng.seen[key] = val

    def _mark(self, tok, reads, writes):
        for w in writes:
            w.w = tok
            w.r = {}
        for r in reads:
            key, sem, val = tok
            if key not in r.r or r.r[key][1] < val:
                r.r[key] = (sem, val)

    def _emit_op(self, o):
        eng, reads, writes = o["eng"], o["reads"], o["writes"]
        self._wait(eng, reads, writes)
        ins = getattr(eng.e, o["name"])(*o["a"], **o["k"])
        if o["signal"]:
            eng.count += 1
            ins.then_inc(eng.sem, 1)
            tok = (eng.name, eng.sem, eng.count)
        else:
            tok = (eng.name, eng.sem, eng.count + 1)
        self._mark(tok, reads, writes)

    def _emit_dma(self, o):
        q, eng, reads, writes = o["q"], o["eng"], o["reads"], o["writes"]
        n = self.qn[q]
        slot, prev = n % self.KQ, n // self.KQ
        sem = self.qsems[q][slot]
        key = f"dq_{q}{slot}"
        if prev > 0 and eng.seen.get(key, 0) < 16 * prev:
            eng.e.wait_ge(sem, 16 * prev)
            eng.seen[key] = 16 * prev
        self._wait(eng, reads, writes)
        eng.e.dma_start(out=o["out"], in_=o["in_"]).then_inc(sem, 16)
        tok = (key, sem, 16 * (prev + 1))
        self.qn[q] = n + 1
        self._mark(tok, reads, writes)
        if o["is_output"]:
            self.out_tokens.append(tok)

    def finish(self):
        best = {}
        for key, sem, val in self.out_tokens:
            if key not in best or best[key][1] < val:
                best[key] = (sem, val)
        for key, (sem, val) in best.items():
            self.sp.e.wait_ge(sem, val)


def _groups():
    return [(0, 1)] + [(1 + 4 * i, 4) for i in range(4)]


def build_nc(plan=None, debug=False):
    nc = bass.Bass("TRN2", target_bir_lowering=False)
    dram_in = lambda n, s: nc.dram_tensor(n, list(s), F32, kind="ExternalInput").ap()
    x_d = dram_in("x", (SEQ, D))
    meta_d = dram_in("meta", (16, D))
    wq_d = dram_in("w_q", (D, 1024)); wk2_d = dram_in("w_k2", (D, 512)); wv_d = dram_in("w_v", (D, 256))
    wza_d = dram_in("w_za", (D, 1024)); wzs_d = dram_in("w_zs", (D, 2048)); wxbc_d = dram_in("w_xbc", (D, 3072))
    wdt_d = dram_in("w_dt", (D, 32)); wga_d = dram_in("w_ga", (D, 1024)); wgs_d = dram_in("w_gs", (D, 1024))
    woa_d = dram_in("w_oa", (1024, D)); wos_d = dram_in("w_os", (2048, D)); wo_d = dram_in("w_o", (D, D))
    gpre_d = dram_in("gpre_tab", (128, 8)); gpost_d = dram_in("gpost", (1, D))
    dtb_d = dram_in("dt_bias", (1, 32)); alog_d = dram_in("a_log", (1, 32)); dsk_d = dram_in("d_skip", (1, 32))
    sink_d = dram_in("sink_tab", (128, 8)); gn_d = dram_in("gn_tab", (128, 16)); dtab_d = dram_in("dsk_tab", (128, 16))
    cw_d = dram_in("cw_tab", (128, 24 * 4)); cb_d = dram_in("cb_tab", (128, 24))
    ident_d = dram_in("c_ident", (128, 128)); tri_d = dram_in("c_tri", (128, 128))
    mtab_d = dram_in("c_mtab", (128, 8 * 512))
    out_d = nc.dram_tensor("out", [SEQ, D], F32, kind="ExternalOutput").ap()
    csd = nc.dram_tensor("csd", [NB, 32, 128], F32, kind="Internal").ap()
    wos_s = nc.dram_tensor("wos_s", [2048, D], F32, kind="Internal").ap()
    dbg_d = {}
    if debug:
        for n, s in [("d_uT", (128, 8 * 512)), ("d_yatt", (128, 8 * 512)), ("d_xsT", (128, 16 * 512)),
                     ("d_BT", (128, 4 * 512)), ("d_yn", (128, 16 * 512)), ("d_merged", (128, 8 * 512)),
                     ("d_qT", (128, 8 * 512)), ("d_mA", (128, 8 * 512))]:
            dbg_d[n] = nc.dram_tensor(n, list(s), F32, kind="ExternalOutput").ap()

    with ExitStack() as es:
        tr = T(nc, es)
        PE, ACT, DVE, POOL = tr.pe, tr.act, tr.dve, tr.pool

        def sb(name, shape, dt):
            return Buf(es.enter_context(nc.sbuf_tensor("sb_" + name, list(shape), dt)))

        ident = sb("ident", (128, 128), BF16)
        tri = sb("tri", (128, 128), F32)
        ones = sb("ones", (128, 128), BF16)
        mtab = sb("mtab", (128, 8, 512), BF16)
        gpre = sb("gpre", (128, 8), F32)
        gpost = sb("gpost", (128, D), F32)
        dtb = sb("dtb", (128, 32), F32)
        Abc = sb("Abc", (128, 32), F32)
        Dbc = sb("Dbc", (128, 32), F32)
        DI = sb("DI", (128, 16, 128), BF16)
        dtab = sb("dtab", (128, 16), F32)
        esink = sb("esink", (128, 8), F32)
        gn = sb("gn", (128, 16), F32)
        cw = sb("cw", (128, 24, 4), F32)
        cb = sb("cb", (128, 24), F32)
        identf = sb("identf", (128, 128), F32)

        NTM = 512
        uT = sb("uT", (128, 8, NTM), BF16)
        bufA = sb("bufA", (128, 8, NTM), BF16)
        bufB = sb("bufB", (128, 8, NTM), BF16)
        kT2 = sb("kT2", (128, 4, 128 + NTM), BF16)
        kmT = sb("kmT", (128, 4, 16), BF16)
        Vt = sb("Vt", (128, 5, 256), BF16)
        Vm = sb("Vm", (16, 256), BF16)
        zsT = sb("zsT", (128, 16, NTM), BF16)
        xsT = sb("xsT", (128, 16, NTM), BF16)
        BT = sb("BT", (128, 4, NTM), BF16)
        CT = sb("CT", (128, 4, NTM), BF16)
        NW = 3
        wbufs = [sb(f"wbuf{i}", (128, 8, 512), BF16) for i in range(NW)]
        raw = [sb(f"raw{i}", (128, 3 + NTM), F32) for i in range(CFG["rot_raw"])]
        halo = sb("halo", (128, 24, 3), F32)
        cacc = [sb(f"cacc{i}", (128, NTM), F32) for i in range(CFG["rot_cacc"])]
        xin = [sb(f"xin{i}", (128, D), F32) for i in range(2)]
        xres = [sb(f"xres{i}", (128, D), F32) for i in range(2)]
        utm = [sb(f"utm{i}", (128, D), BF16) for i in range(1)]
        stat = [sb(f"stat{i}", (128, 8), F32) for i in range(2)]
        PT = [[sb(f"PT{i}{e}", (128, 512), BF16) for e in range(2)] for i in range(2)]
        PTm = [sb(f"PTm{i}", (16, 512), BF16) for i in range(2)]
        den = [sb(f"den{i}", (128, 256), F32) for i in range(CFG["rot_den"])]
        atmp = [sb(f"atmp{i}", (128, 256), F32) for i in range(2)]
        mtmp = [sb(f"mtmp{i}", (128, NTM), F32) for i in range(CFG["rot_mtmp"])]
        dtv = [sb(f"dtv{i}", (128, 32), F32) for i in range(2)]
        dte = [sb(f"dte{i}", (128, 32), F32) for i in range(2)]
        dtt = [sb(f"dtt{i}", (128, 32), F32) for i in range(2)]
        lndt = [sb(f"lndt{i}", (128, 32), F32) for i in range(2)]
        av = [sb(f"av{i}", (128, 32), F32) for i in range(2)]
        cs_sb = [sb(f"cs{i}", (128, 32), F32) for i in range(2)]
        bias1 = [sb(f"bias1{i}", (128, 32), F32) for i in range(4)]
        ecs = [sb(f"ecs{i}", (128, 32), F32) for i in range(4)]
        csT_sb = [sb(f"csT{i}", (32, 128), F32) for i in range(2)]
        csbc = [sb(f"csbc{i}", (128, 8, 128), F32) for i in range(CFG["rot_csbc"])]
        darg = [sb(f"darg{i}", (128, 8), F32) for i in range(2)]
        dsdt = [sb(f"dsdt{i}", (128, 8), F32) for i in range(2)]
        cdbc = [sb(f"cdbc{i}", (128, 8), F32) for i in range(2)]
        CBm = [sb(f"CBm{i}", (128, 4, 128), BF16) for i in range(2)]
        Eb = [sb(f"Eb{i}", (128, 8, 128), BF16) for i in range(CFG["rot_Eb"])]
        xtm = [sb(f"xtm{i}", (128, 512), BF16) for i in range(CFG["rot_xtm"])]
        xrd = [sb(f"xrd{i}", (128, 512), BF16) for i in range(2)]
        Btm = [sb(f"Btm{i}", (128, 128), BF16) for i in range(2)]
        yos = [sb(f"yos{i}", (128, 512), BF16) for i in range(CFG["rot_yos"])]
        Sst = sb("Sst", (128, 2048), F32)
        prevb = sb("prevb", (128, 2048), BF16)
        stmp = [sb(f"stmp{i}", (128, 512), F32) for i in range(CFG["rot_stmp"])]
        ysq = [sb(f"ysq{i}", (128, 4, 128), BF16) for i in range(2)]
        nv = [sb(f"nv{i}", (128, 384), F32) for i in range(2)]
        wtmp = xin

        banks = [Buf(es.enter_context(nc.psum_tensor(f"ps{i}", [128, 512], F32)), excl=True) for i in range(8)]
        bank_i = [0]

        def nbank():
            b = banks[bank_i[0] % 8]
            bank_i[0] += 1
            return b

        cnt = {}

        def rot(lst, name):
            i = cnt.get(name, 0)
            cnt[name] = i + 1
            return lst[i % len(lst)]

        A = "all"
        ENGS = {"act": ACT, "dve": DVE, "pool": POOL}

        def copy_op(which, out, in_, reads, writes, scale=None):
            en = CFG[which]
            if en == "act":
                if scale is None:
                    tr.op(ACT, lambda e: e.activation(out=out, in_=in_, func=AF.Copy), reads=reads, writes=writes)
                else:
                    tr.op(ACT, lambda e: e.activation(out=out, in_=in_, func=AF.Copy, scale=scale), reads=reads, writes=writes)
            else:
                if scale is None:
                    tr.op(ENGS[en], lambda e: e.tensor_copy(out=out, in_=in_), reads=reads, writes=writes)
                else:
                    tr.op(ENGS[en], lambda e: e.tensor_scalar(out=out, in0=in_, scalar1=scale, scalar2=None, op0=ALU.mult),
                          reads=reads, writes=writes)

        def mm(out, lhsT, rhs, start, stop, reads, writes, signal):
            tr.op(PE, lambda e: e.matmul(out, lhsT=lhsT, rhs=rhs, start=start, stop=stop),
                  reads=reads, writes=writes, signal=signal)

        def load_sp(dst_buf, dst_ap, src_ap, key=A):
            tr.dma("sp", dst_ap, src_ap, writes=dst_buf.r(key))

        def load_cast(dst_buf, dst_ap, src_ap, key=A, extra_reads=()):
            tr.dma("pool", dst_ap, src_ap, reads=list(extra_reads), writes=dst_buf.r(key))

        load_cast(ident, ident[:, :], ident_d)
        load_sp(identf, identf[:, :], ident_d)
        load_sp(tri, tri[:, :], tri_d)
        load_cast(mtab, mtab[:, :, :], mtab_d.rearrange("p (a b) -> p a b", a=8))
        load_sp(gpre, gpre[:, :], gpre_d)
        load_sp(gpost, gpost[:, :], gpost_d.partition_broadcast(128))
        load_sp(dtb, dtb[:, :], dtb_d.partition_broadcast(128))
        load_sp(Abc, Abc[:, :], alog_d.partition_broadcast(128))
        load_sp(Dbc, Dbc[:, :], dsk_d.partition_broadcast(128))
        load_sp(esink, esink[:, :], sink_d)
        load_sp(gn, gn[:, :], gn_d)
        load_sp(cw, cw[:, :, :], cw_d.rearrange("p (c k) -> p c k", k=4))
        load_sp(cb, cb[:, :], cb_d)
        tr.op(POOL, lambda e: e.memset(ones[:, :], 1.0), writes=ones.r(A))
        tr.op(POOL, lambda e: e.memset(halo[:, :, :], 0.0), writes=halo.r(*range(24)))
        tr.op(POOL, lambda e: e.memset(Sst[:, :], 0.0), writes=Sst.r(A))
        tr.op(POOL, lambda e: e.memset(prevb[:, :], 0.0), writes=prevb.r(A))
        tr.op(ACT, lambda e: e.activation(out=Abc[:, :], in_=Abc[:, :], func=AF.Exp), reads=Abc.r(A), writes=Abc.r(A))
        tr.op(DVE, lambda e: e.tensor_scalar(out=Abc[:, :], in0=Abc[:, :], scalar1=-1.0, scalar2=None, op0=ALU.mult),
              reads=Abc.r(A), writes=Abc.r(A))
        tr.op(ACT, lambda e: e.activation(out=esink[:, :], in_=esink[:, :], func=AF.Exp), reads=esink.r(A), writes=esink.r(A))
        load_sp(dtab, dtab[:, :], dtab_d)
        tr.op(DVE, lambda e: e.tensor_tensor(out=DI[:, :, :],
                                             in0=identf[:, :].unsqueeze(1).broadcast_to([128, 16, 128]),
                                             in1=dtab[:, :].unsqueeze(2).broadcast_to([128, 16, 128]), op=ALU.mult),
              reads=identf.r(A) + dtab.r(A), writes=DI.r(A))
        wos_sB = Buf(wos_s)
        def wos_prep():
            for kc in range(16):
                wt = rot(xres, "xres")
                tr.dma("sp", wt[:, :], wos_d[kc * 128:(kc + 1) * 128, :], reads=zsT.r((15, 0)), writes=wt.r(A))
                tr.op(DVE, lambda e: e.tensor_scalar(out=wt[:, :], in0=wt[:, :], scalar1=gn[:, kc:kc + 1], scalar2=None,
                                                      op0=ALU.mult), reads=wt.r(A) + gn.r(A), writes=wt.r(A))
                tr.dma("sp", wos_s[kc * 128:(kc + 1) * 128, :], wt[:, :], reads=wt.r(A), writes=wos_sB.r(kc // 8))

        csdB = Buf(csd)

        wdram = {"wq": wq_d, "wk2": wk2_d, "wza": wza_d, "wzs": wzs_d, "wxbc": wxbc_d, "wga": wga_d, "wgs": wgs_d,
                 "woa": woa_d, "wo": wo_d}
        wrec = []
        wstate = {"i": 0, "issued": 0}
        WDEPTH = 1

        def w_issue(j, key):
            wb = wbufs[j % NW]
            kind = key[0]
            if kind == "wvdt":
                load_cast(wb, wb[:, :, 0:256], wview(wv_d, 256, 0, 256))
                load_cast(wb, wb[:, :, 256:288], wview(wdt_d, 32, 0, 32))
            elif kind == "wos":
                _, kh, mh = key
                load_cast(wb, wb[:, :, :], wos_s[kh * 1024:(kh + 1) * 1024, :].rearrange("(kc p) n -> p kc n", p=128)[:, :, mh * 512:(mh + 1) * 512],
                          extra_reads=wos_sB.r(kh))
            else:
                _, c0, c1 = key
                load_cast(wb, wb[:, :, 0:c1 - c0], wview(wdram[kind], 0, c0, c1))

        def wget(key):
            i = wstate["i"]
            wstate["i"] = i + 1
            wrec.append(key)
            todo = [(i, key)]
            for j, k_ in todo:
                w_issue(j, k_)
                wstate["issued"] = j + 1
            return wbufs[i % NW]


        def wview(wd, ncols_total, c0, c1):
            return wd.rearrange("(kc p) n -> p kc n", p=128)[:, :, c0:c1]

        def interleave(items):
            st = [[g, 0, max(1, n)] for g, n in items]
            while st:
                st.sort(key=lambda s: s[1] / s[2])
                s = st[0]
                try:
                    next(s[0])
                    s[1] += 1
                except StopIteration:
                    st.remove(s)

        def run(gen):
            for _ in gen:
                pass

        def make_group(gi, b0, nb):
            NT = 128 * nb
            first = (gi == 0)
            blks = list(range(nb))
            uT_all = uT.r(*blks)
            u_src = lambda kc: uT[:, kc, 0:NT]
            G = {}

            def proj_fm(wname, ncols, src, evac, src_reads, nk=8):
                for wc in range((ncols + 511) // 512):
                    c0 = wc * 512
                    cw_ = min(512, ncols - c0)
                    wb = wget((wname, c0, c0 + cw_))
                    for m in range(cw_ // 128):
                        pb = nbank()
                        for kc in range(nk):
                            mm(pb[:, 0:NT], wb[:, kc, m * 128:(m + 1) * 128], src(kc), kc == 0, kc == nk - 1,
                               reads=wb.r(A) + src_reads, writes=pb.r(A), signal=(kc == nk - 1))
                        evac(wc * 4 + m, pb)
                        yield

            def s0():
                for bl in blks:
                    b = b0 + bl
                    xi = rot(xin, "xin")
                    st = rot(stat, "stat")
                    ut = rot(utm, "utm")
                    if b == 0:
                        tr.op(DVE, lambda e: e.memset(xi[:, :], 0.0), writes=xi.r(A))
                        tr.dma("sp", xi[112:128, :], meta_d, writes=xi.r(A))
                    else:
                        tr.dma("sp", xi[:, :], x_d[(b - 1) * 128:b * 128, :], writes=xi.r(A))
                    tr.op(DVE, lambda e: e.memset(st[:, 0:1], 0.0), writes=st.r(A))
                    tr.op(ACT, lambda e: e.activation(out=ut[:, :], in_=xi[:, :], func=AF.Square, accum_out=st[:, 0:1]),
                          reads=xi.r(A) + st.r(A), writes=ut.r(A) + st.r(A))
                    tr.op(DVE, lambda e: e.tensor_scalar(out=st[:, 1:2], in0=st[:, 0:1], scalar1=1.0 / D, scalar2=EPS,
                                                         op0=ALU.mult, op1=ALU.add), reads=st.r(A), writes=st.r(A))
                    tr.op(ACT, lambda e: e.activation(out=st[:, 2:3], in_=st[:, 1:2], func=AF.Ln), reads=st.r(A), writes=st.r(A))
                    tr.op(ACT, lambda e: e.activation(out=st[:, 3:4], in_=st[:, 2:3], func=AF.Exp, scale=-0.5),
                          reads=st.r(A), writes=st.r(A))
                    tr.op(DVE, lambda e: e.tensor_scalar(out=ut[:, :], in0=xi[:, :], scalar1=st[:, 3:4], scalar2=None,
                                                         op0=ALU.mult), reads=xi.r(A) + st.r(A), writes=ut.r(A))
                    pb = nbank()
                    pbv = pb[:, :].bitcast(BF16)
                    for kc in range(8):
                        tr.op(PE, lambda e: e.transpose(pbv[:, kc * 128:(kc + 1) * 128], ut[:, kc * 128:(kc + 1) * 128], ident[:, :]),
                              reads=ut.r(A) + ident.r(A), writes=pb.r(A), signal=(kc == 7))
                    tr.op(DVE, lambda e: e.tensor_tensor(out=uT[:, :, bl * 128:(bl + 1) * 128],
                                                         in0=pbv.rearrange("p (c t) -> p c t", c=8),
                                                         in1=gpre[:, :].unsqueeze(2).broadcast_to([128, 8, 128]), op=ALU.mult),
                          reads=pb.r(A) + gpre.r(A), writes=uT.r(bl))
                    yield
            G["s0"] = s0

            def proj1():
                def ev_k(c, pb):
                    copy_op("k", kT2[:, c, 128:128 + NT], pb[:, 0:NT], pb.r(A), kT2.r(A))
                yield from proj_fm("wk2", 512, u_src, ev_k, uT_all)
                if first:
                    tr.op(DVE, lambda e: e.tensor_copy(out=kmT[:, :, :], in_=kT2[:, :, 128 + 112:128 + 128]),
                          reads=kT2.r(A), writes=kmT.r(A))
                wbv = wget(("wvdt",))
                for bl in blks:
                    pb = nbank()
                    for kc in range(8):
                        mm(pb[:, 0:256], uT[:, kc, bl * 128:(bl + 1) * 128], wbv[:, kc, 0:256], kc == 0, kc == 7,
                           reads=wbv.r(A) + uT.r(bl), writes=pb.r(A), signal=(kc == 7))
                    copy_op("v", Vt[:, 1 + bl, :], pb[:, 0:256], pb.r(A), Vt.r(1 + bl))
                for bl in blks:
                    b = b0 + bl
                    k = rot([0, 1], "dtk")
                    pb = nbank()
                    for kc in range(8):
                        mm(pb[:, 0:32], uT[:, kc, bl * 128:(bl + 1) * 128], wbv[:, kc, 256:288], kc == 0, kc == 7,
                           reads=wbv.r(A) + uT.r(bl), writes=pb.r(A), signal=(kc == 7))
                    tr.op(DVE, lambda e: e.tensor_tensor(out=dtv[k][:, :], in0=pb[:, 0:32], in1=dtb[:, :], op=ALU.add),
                          reads=pb.r(A) + dtb.r(A), writes=dtv[k].r(A))
                    tr.op(ACT, lambda e: e.activation(out=dte[k][:, :], in_=dtv[k][:, :], func=AF.Exp),
                          reads=dtv[k].r(A), writes=dte[k].r(A))
                    tr.op(ACT, lambda e: e.activation(out=dtt[k][:, :], in_=dte[k][:, :], func=AF.Ln, bias=1.0),
                          reads=dte[k].r(A), writes=dtt[k].r(A))
                    tr.op(ACT, lambda e: e.activation(out=lndt[k][:, :], in_=dtt[k][:, :], func=AF.Ln),
                          reads=dtt[k].r(A), writes=lndt[k].r(A))
                    tr.op(DVE, lambda e: e.tensor_tensor(out=av[k][:, :], in0=dtt[k][:, :], in1=Abc[:, :], op=ALU.mult),
                          reads=dtt[k].r(A) + Abc.r(A), writes=av[k].r(A))
                    pc = nbank()
                    mm(pc[:, 0:32], tri[:, :], av[k][:, :], True, True, reads=tri.r(A) + av[k].r(A), writes=pc.r(A), signal=False)
                    mm(pc[0:32, 128:256], av[k][:, :], tri[:, :], True, True, reads=tri.r(A) + av[k].r(A), writes=pc.r(A), signal=True)
                    tr.op(DVE, lambda e: e.tensor_copy(out=cs_sb[k][:, :], in_=pc[:, 0:32]), reads=pc.r(A), writes=cs_sb[k].r(A))
                    tr.op(DVE, lambda e: e.tensor_copy(out=csT_sb[k][:, :], in_=pc[0:32, 128:256]), reads=pc.r(A), writes=csT_sb[k].r(A))
                    tr.dma("sp", csd[b], csT_sb[k][:, :], reads=csT_sb[k].r(A), writes=csdB.r(b))
                    tr.op(DVE, lambda e: e.tensor_tensor(out=bias1[bl][:, :], in0=lndt[k][:, :], in1=cs_sb[k][:, :], op=ALU.subtract),
                          reads=lndt[k].r(A) + cs_sb[k].r(A), writes=bias1[bl].r(A))
                    tr.op(ACT, lambda e: e.activation(out=ecs[bl][:, :], in_=cs_sb[k][:, :], func=AF.Exp),
                          reads=cs_sb[k].r(A), writes=ecs[bl].r(A))
                if first:
                    pb = nbank()
                    for kc in range(8):
                        mm(pb[0:16, 0:256], uT[:, kc, 112:128], wbv[:, kc, 0:256], kc == 0, kc == 7,
                           reads=wbv.r(A) + uT.r(0), writes=pb.r(A), signal=(kc == 7))
                    tr.op(ACT, lambda e: e.activation(out=Vm[:, :], in_=pb[0:16, 0:256], func=AF.Copy),
                          reads=pb.r(A), writes=Vm.r(A))
                yield
                if not first:
                    def ev_za(c, pb):
                        tr.op(ACT, lambda e: e.activation(out=bufB[:, c, 0:NT], in_=pb[:, 0:NT], func=AF.Silu),
                              reads=pb.r(A), writes=bufB.r(c))
                    yield from proj_fm("wza", 1024, u_src, ev_za, uT_all)

                    def ev_zs(c, pb):
                        tr.op(ACT, lambda e: e.activation(out=zsT[:, c, 0:NT], in_=pb[:, 0:NT], func=AF.Silu),
                              reads=pb.r(A), writes=zsT.r(*[(c, bl) for bl in blks]))
                    yield from proj_fm("wzs", 2048, u_src, ev_zs, uT_all)

                    def ev_q(c, pb):
                        copy_op("q", bufA[:, c, 0:NT], pb[:, 0:NT], pb.r(A), bufA.r(c), scale=0.125)
                    yield from proj_fm("wq", 1024, u_src, ev_q, uT_all)
            G["proj1"] = proj1

            def attn():
                for bl in blks:
                    b = b0 + bl
                    kbs = [1] if b == 1 else [0, 1]
                    tok = slice(bl * 128, (bl + 1) * 128)
                    for g in range(4):
                        stb = [nbank(), nbank()]
                        smb = nbank()
                        pt = rot(PT, "PT")
                        ptm = rot(PTm, "PTm")
                        c0 = 0 if b != 1 else 256
                        for e_ in range(2):
                            ps = slice(e_ * 64, (e_ + 1) * 64)
                            mm(stb[e_][:, c0:512], ident[:, :], mtab[:, 2 * g + e_, c0:512], True, False,
                               reads=ident.r(A) + mtab.r(A), writes=stb[e_].r(A), signal=False)
                            for kb in kbs:
                                kc0 = bl * 128 + kb * 128
                                mm(stb[e_][:, kb * 256:(kb + 1) * 256].rearrange("p (j q) -> p j q", j=2),
                                   kT2[ps, g, kc0:kc0 + 128], bufA[ps, 2 * g:2 * g + 2, tok], False, kb == 1,
                                   reads=kT2.r(A) + bufA.r(2 * g, 2 * g + 1), writes=stb[e_].r(A), signal=(kb == 1))
                            mm(smb[0:16, e_ * 256:(e_ + 1) * 256].rearrange("p (j q) -> p j q", j=2),
                               kmT[ps, g, :], bufA[ps, 2 * g:2 * g + 2, tok], True, True,
                               reads=kmT.r(A) + bufA.r(2 * g, 2 * g + 1), writes=smb.r(A), signal=(e_ == 1))
                        for e_ in range(2):
                            tr.op(ACT, lambda e: e.activation(out=pt[e_][:, c0:512], in_=stb[e_][:, c0:512], func=AF.Exp),
                                  reads=stb[e_].r(A), writes=pt[e_].r(A))
                        tr.op(ACT, lambda e: e.activation(out=ptm[:, :], in_=smb[0:16, :], func=AF.Exp),
                              reads=smb.r(A), writes=ptm.r(A))
                        ob = nbank()
                        for part in range(2):
                            for e_ in range(2):
                                ps = slice(e_ * 64, (e_ + 1) * 64)
                                oc = slice(part * 256, (part + 1) * 256)
                                for i, kb in enumerate(kbs):
                                    slot = bl + kb
                                    lhs = Vt[:, slot, g * 64:(g + 1) * 64] if part == 0 else ones[:, 0:64]
                                    mm(ob[ps, oc], lhs, pt[e_][:, kb * 256:(kb + 1) * 256], i == 0, False,
                                       reads=Vt.r(slot) + ones.r(A) + pt[e_].r(A), writes=ob.r(A), signal=False)
                                lhs = Vm[0:16, g * 64:(g + 1) * 64] if part == 0 else ones[0:16, 0:64]
                                mm(ob[ps, oc], lhs, ptm[0:16, e_ * 256:(e_ + 1) * 256], False, True,
                                   reads=Vm.r(A) + ones.r(A) + ptm.r(A), writes=ob.r(A), signal=(part == 1 and e_ == 1))
                        dn = rot(den, "den")
                        at = rot(atmp, "atmp")
                        tr.op(DVE, lambda e: e.tensor_tensor(out=dn[:, :].rearrange("p (j q) -> p j q", j=2),
                                                             in0=ob[:, 256:512].rearrange("p (j q) -> p j q", j=2),
                                                             in1=esink[:, 2 * g:2 * g + 2].unsqueeze(2).broadcast_to([128, 2, 128]),
                                                             op=ALU.add),
                              reads=ob.r(A) + esink.r(A), writes=dn.r(A))
                        tr.op(DVE, lambda e: e.reciprocal(out=dn[:, :], in_=dn[:, :]), reads=dn.r(A), writes=dn.r(A))
                        tr.op(DVE, lambda e: e.tensor_tensor(out=at[:, :], in0=ob[:, 0:256], in1=dn[:, :], op=ALU.mult),
                              reads=ob.r(A) + dn.r(A), writes=at.r(A))
                        tr.op(DVE, lambda e: e.tensor_tensor(out=bufB[:, 2 * g:2 * g + 2, tok],
                                                             in0=at[:, :].rearrange("p (j q) -> p j q", j=2),
                                                             in1=bufB[:, 2 * g:2 * g + 2, tok], op=ALU.mult),
                              reads=at.r(A) + bufB.r(2 * g, 2 * g + 1), writes=bufB.r(2 * g, 2 * g + 1))
                        yield
                slide()
            G["attn"] = attn

            def slide():
                tr.op(POOL, lambda e: e.tensor_copy(out=kT2[:, :, 0:128], in_=kT2[:, :, NT:NT + 128]),
                      reads=kT2.r(A), writes=kT2.r(A))
                tr.op(POOL, lambda e: e.tensor_copy(out=Vt[:, 0, :], in_=Vt[:, nb, :]), reads=Vt.r(nb), writes=Vt.r(0))
            G["slide"] = slide

            def xbc():
                for wc in range(6):
                    wb = wget(("wxbc", wc * 512, (wc + 1) * 512))
                    for m in range(4):
                        c24 = wc * 4 + m
                        rw = rot(raw, "raw")
                        ca = rot(cacc, "cacc")
                        pb = nbank()
                        for kc in range(8):
                            mm(pb[:, 0:NT], wb[:, kc, m * 128:(m + 1) * 128], uT[:, kc, 0:NT], kc == 0, kc == 7,
                               reads=wb.r(A) + uT_all, writes=pb.r(A), signal=(kc == 7))
                        tr.op(ENGS[CFG["halo"]], lambda e: e.tensor_copy(out=rw[:, 0:3], in_=halo[:, c24, :]), reads=halo.r(c24), writes=rw.r(A))
                        copy_op("raw", rw[:, 3:3 + NT], pb[:, 0:NT], pb.r(A), rw.r(A))
                        tr.op(ENGS[CFG["halo"]], lambda e: e.tensor_copy(out=halo[:, c24, :], in_=rw[:, NT:NT + 3]), reads=rw.r(A), writes=halo.r(c24))
                        eng = DVE
                        tr.op(ACT, lambda e: e.activation(out=ca[:, 0:NT], in_=pb[:, 0:NT], func=AF.Identity,
                                                          scale=cw[:, c24, 3:4], bias=cb[:, c24:c24 + 1]),
                              reads=pb.r(A) + cw.r(A) + cb.r(A), writes=ca.r(A))
                        if c24 < 16:
                            dst, dc = xsT, c24
                        elif c24 < 20:
                            dst, dc = BT, c24 - 16
                        else:
                            dst, dc = CT, c24 - 20
                        for kk in (2, 1):
                            tr.op(eng, lambda e: e.scalar_tensor_tensor(out=ca[:, 0:NT], in0=rw[:, kk:kk + NT],
                                                                        scalar=cw[:, c24, kk:kk + 1], in1=ca[:, 0:NT],
                                                                        op0=ALU.mult, op1=ALU.add),
                                  reads=rw.r(A) + cw.r(A) + ca.r(A), writes=ca.r(A))
                        tr.op(eng, lambda e: e.scalar_tensor_tensor(out=dst[:, dc, 0:NT], in0=rw[:, 0:NT],
                                                                    scalar=cw[:, c24, 0:1], in1=ca[:, 0:NT],
                                                                    op0=ALU.mult, op1=ALU.add),
                              reads=rw.r(A) + cw.r(A) + ca.r(A), writes=dst.r(dc))
                        yield
                for dst, ncn in ((xsT, 16), (BT, 4), (CT, 4)):
                    tr.op(ACT, lambda e: e.activation(out=dst[:, :, 0:NT], in_=dst[:, :, 0:NT], func=AF.Silu),
                          reads=dst.r(*range(ncn)), writes=dst.r(*range(ncn)))
                    if first:
                        tr.op(DVE, lambda e: e.memset(dst[:, :, 0:112], 0.0), writes=dst.r(*range(ncn)))
            G["xbc"] = xbc

            def scan():
                for bl in blks:
                    b = b0 + bl
                    k = bl
                    tok = slice(bl * 128, (bl + 1) * 128)
                    pcb = nbank()
                    for g in range(4):
                        mm(pcb[:, g * 128:(g + 1) * 128], BT[:, g, tok], CT[:, g, tok], True, True,
                           reads=BT.r(g) + CT.r(g), writes=pcb.r(A), signal=(g == 3))
                    cbm = rot(CBm, "CBm")
                    tr.op(DVE, lambda e: e.tensor_tensor(out=cbm[:, :, :], in0=pcb[:, :].rearrange("p (g l) -> p g l", g=4),
                                                         in1=tri[:, :].unsqueeze(1).broadcast_to([128, 4, 128]), op=ALU.mult),
                          reads=pcb.r(A) + tri.r(A), writes=cbm.r(A))
                    for g in range(4):
                        hs = slice(g * 8, (g + 1) * 8)
                        xs_r = xsT.r(*range(g * 4, g * 4 + 4))
                        cbc = rot(csbc, "csbc")
                        tr.dma("sp", cbc[:, :, :], csd[b, g * 8:(g + 1) * 8, :].partition_broadcast(128),
                               reads=csdB.r(b), writes=cbc.r(A))
                        da = rot(darg, "darg"); dd = rot(dsdt, "dsdt"); cd = rot(cdbc, "cdbc")
                        tr.op(DVE, lambda e: e.tensor_tensor(out=da[:, :], in0=cbc[:, :, 127], in1=bias1[k][:, hs], op=ALU.add),
                              reads=cbc.r(A) + bias1[k].r(A), writes=da.r(A))
                        tr.op(ACT, lambda e: e.activation(out=dd[:, :], in_=da[:, :], func=AF.Exp), reads=da.r(A), writes=dd.r(A))
                        tr.op(ACT, lambda e: e.activation(out=cd[:, :], in_=cbc[:, :, 127], func=AF.Exp), reads=cbc.r(A), writes=cd.r(A))
                        eb = rot(Eb, "Eb"); lt = eb
                        tr.op(ENGS[CFG["ebias"]], lambda e: e.tensor_tensor(out=cbc[:, :, :], in0=cbc[:, :, :],
                                                              in1=bias1[k][:, hs].unsqueeze(2).broadcast_to([128, 8, 128]), op=ALU.add),
                              reads=cbc.r(A) + bias1[k].r(A), writes=cbc.r(A))
                        tr.op(ACT, lambda e: e.activation(out=eb[:, :, :], in_=cbc[:, :, :], func=AF.Exp),
                              reads=cbc.r(A), writes=eb.r(A))
                        tr.op(DVE, lambda e: e.scalar_tensor_tensor(out=lt[:, :, :], in0=eb[:, :, :], scalar=BIG,
                                                                    in1=cbm[:, g, :].unsqueeze(1).broadcast_to([128, 8, 128]),
                                                                    op0=ALU.min, op1=ALU.mult),
                              reads=eb.r(A) + cbm.r(A), writes=lt.r(A))
                        ptx = nbank()
                        ptxv = ptx[:, :].bitcast(BF16)
                        for c4 in range(4):
                            tr.op(PE, lambda e: e.transpose(ptxv[:, c4 * 128:(c4 + 1) * 128], xsT[:, g * 4 + c4, tok], ident[:, :]),
                                  reads=xs_r + ident.r(A), writes=ptx.r(A), signal=False)
                        tr.op(PE, lambda e: e.transpose(ptxv[:, 512:640], BT[:, g, tok], ident[:, :]),
                              reads=BT.r(g) + ident.r(A), writes=ptx.r(A), signal=True)
                        xt = rot(xtm, "xtm"); xr = rot(xrd, "xrd"); bt = rot(Btm, "Btm")
                        copy_op("xt", xt[:, :], ptxv[:, 0:512], ptx.r(A), xt.r(A))
                        copy_op("bt", bt[:, 0:128], ptxv[:, 512:640], ptx.r(A), bt.r(A))
                        tr.op(DVE, lambda e: e.tensor_tensor(out=xr[:, :].rearrange("p (h d) -> p h d", h=8),
                                                             in0=ptxv[:, 0:512].rearrange("p (h d) -> p h d", h=8),
                                                             in1=dd[:, :].unsqueeze(2).broadcast_to([128, 8, 64]), op=ALU.mult),
                              reads=ptx.r(A) + dd.r(A), writes=xr.r(A))
                        if b > 0:
                            pyo = nbank()
                            mm(pyo[:, :], CT[:, g, tok], prevb[:, g * 512:(g + 1) * 512], True, True,
                               reads=CT.r(g) + prevb.r(g), writes=pyo.r(A), signal=True)
                            yo = rot(yos, "yos")
                            tr.op(DVE, lambda e: e.tensor_tensor(out=yo[:, :].rearrange("p (h d) -> p h d", h=8),
                                                                 in0=pyo[:, :].rearrange("p (h d) -> p h d", h=8),
                                                                 in1=ecs[k][:, hs].unsqueeze(2).broadcast_to([128, 8, 64]), op=ALU.mult),
                                  reads=pyo.r(A) + ecs[k].r(A), writes=yo.r(A))
                            py = nbank()
                            for c4 in range(4):
                                osl = slice(c4 * 128, (c4 + 1) * 128)
                                for e_ in range(2):
                                    r = c4 * 2 + e_
                                    h = g * 8 + r
                                    ps = slice(e_ * 64, (e_ + 1) * 64)
                                    mm(py[ps, osl], xt[:, r * 64:(r + 1) * 64], lt[:, r, :], True, False,
                                       reads=xt.r(A) + lt.r(A), writes=py.r(A), signal=False)
                                mm(py[:, osl], DI[:, g * 4 + c4, :], xsT[:, g * 4 + c4, tok], False, False,
                                   reads=DI.r(A) + xs_r, writes=py.r(A), signal=False)
                                mm(py[:, osl], yo[:, c4 * 128:(c4 + 1) * 128], ident[:, :], False, True,
                                   reads=yo.r(A) + ident.r(A), writes=py.r(A), signal=(c4 == 3))
                            zk = [(g * 4 + c4, bl) for c4 in range(4)]
                            tr.op(DVE, lambda e: e.tensor_tensor(out=zsT[:, g * 4:(g + 1) * 4, tok],
                                                                 in0=py[:, :].rearrange("p (c l) -> p c l", c=4),
                                                                 in1=zsT[:, g * 4:(g + 1) * 4, tok], op=ALU.mult),
                                  reads=py.r(A) + zsT.r(*zk), writes=zsT.r(*zk))
                            yq = rot(ysq, "ysq")
                            tr.op(ACT, lambda e: e.activation(out=yq[:, :, :], in_=zsT[:, g * 4:(g + 1) * 4, tok], func=AF.Square),
                                  reads=zsT.r(*zk), writes=yq.r(A))
                            pn = nbank()
                            for c4 in range(4):
                                mm(pn[:, 0:128], ones[:, :], yq[:, c4, :], c4 == 0, c4 == 3,
                                   reads=ones.r(A) + yq.r(A), writes=pn.r(A), signal=(c4 == 3))
                            nvv = rot(nv, "nv")
                            tr.op(DVE, lambda e: e.tensor_scalar(out=nvv[:, 0:128], in0=pn[:, 0:128], scalar1=1.0 / 512, scalar2=EPS,
                                                                 op0=ALU.mult, op1=ALU.add), reads=pn.r(A), writes=nvv.r(A))
                            tr.op(ACT, lambda e: e.activation(out=nvv[:, 128:256], in_=nvv[:, 0:128], func=AF.Ln),
                                  reads=nvv.r(A), writes=nvv.r(A))
                            tr.op(ACT, lambda e: e.activation(out=nvv[:, 256:384], in_=nvv[:, 128:256], func=AF.Exp, scale=-0.5),
                                  reads=nvv.r(A), writes=nvv.r(A))
                            tr.op(ENGS[CFG["nmul"]], lambda e: e.tensor_tensor(out=zsT[:, g * 4:(g + 1) * 4, tok],
                                                                 in0=zsT[:, g * 4:(g + 1) * 4, tok],
                                                                 in1=nvv[:, 256:384].unsqueeze(1).broadcast_to([128, 4, 128]), op=ALU.mult),
                                  reads=nvv.r(A) + zsT.r(*zk), writes=zsT.r(*zk))
                        pst = nbank()
                        mm(pst[:, :], bt[:, 0:128], xr[:, :], True, True, reads=bt.r(A) + xr.r(A), writes=pst.r(A), signal=True)
                        sm = rot(stmp, "stmp")
                        gs_ = slice(g * 512, (g + 1) * 512)
                        tr.op(ENGS[CFG["sm"]], lambda e: e.tensor_tensor(out=sm[:, :].rearrange("p (h d) -> p h d", h=8),
                                                              in0=Sst[:, gs_].rearrange("p (h d) -> p h d", h=8),
                                                              in1=cd[:, :].unsqueeze(2).broadcast_to([128, 8, 64]), op=ALU.mult),
                              reads=Sst.r(g) + cd.r(A), writes=sm.r(A))
                        tr.op(DVE, lambda e: e.tensor_tensor(out=Sst[:, gs_], in0=pst[:, :], in1=sm[:, :], op=ALU.add),
                              reads=pst.r(A) + sm.r(A), writes=Sst.r(g))
                        copy_op("prevb", prevb[:, gs_], Sst[:, gs_], Sst.r(g), prevb.r(g))
                        yield
            G["scan"] = scan

            def s3():
                def sig_evac(dst, c, pb):
                    mt = rot(mtmp, "mtmp")
                    tr.op(ACT, lambda e: e.activation(out=mt[:, 0:NT], in_=pb[:, 0:NT], func=AF.Exp, scale=-1.0),
                          reads=pb.r(A), writes=mt.r(A))
                    tr.op(ACT, lambda e: e.activation(out=mt[:, 0:NT], in_=mt[:, 0:NT], func=AF.Ln, bias=1.0),
                          reads=mt.r(A), writes=mt.r(A))
                    tr.op(ACT, lambda e: e.activation(out=dst[:, c, 0:NT], in_=mt[:, 0:NT], func=AF.Exp, scale=-1.0),
                          reads=mt.r(A), writes=dst.r(c))

                def ev_ga(c, pb):
                    sig_evac(bufA, c, pb)
                yield from proj_fm("wga", 1024, u_src, ev_ga, uT_all)

                def ev_oa(c, pb):
                    tr.op(DVE, lambda e: e.tensor_tensor(out=bufA[:, c, 0:NT], in0=pb[:, 0:NT], in1=bufA[:, c, 0:NT], op=ALU.mult),
                          reads=pb.r(A) + bufA.r(c), writes=bufA.r(c))
                yield from proj_fm("woa", 1024, lambda kc: bufB[:, kc, 0:NT], ev_oa, bufB.r(*range(8)))

                def ev_gs(c, pb):
                    sig_evac(bufB, c, pb)
                yield from proj_fm("wgs", 1024, u_src, ev_gs, uT_all)
            G["s3"] = s3

            def s5():
                zs_all = zsT.r(*[(c, bl) for c in range(16) for bl in blks])
                for mh in range(2):
                    pbs = [nbank() for _ in range(4)]
                    for kh in range(2):
                        wb = wget(("wos", kh, mh))
                        for m in range(4):
                            for kc in range(8):
                                mm(pbs[m][:, 0:NT], wb[:, kc, m * 128:(m + 1) * 128], zsT[:, kh * 8 + kc, 0:NT],
                                   kh == 0 and kc == 0, kh == 1 and kc == 7,
                                   reads=wb.r(A) + zs_all, writes=pbs[m].r(A), signal=(kh == 1 and kc == 7))
                    for m in range(4):
                        c = mh * 4 + m
                        mt = rot(mtmp, "mtmp")
                        tr.op(DVE, lambda e: e.tensor_tensor(out=mt[:, 0:NT], in0=pbs[m][:, 0:NT], in1=bufB[:, c, 0:NT], op=ALU.mult),
                              reads=pbs[m].r(A) + bufB.r(c), writes=mt.r(A))
                        tr.op(ENGS[CFG["s5add"]], lambda e: e.tensor_tensor(out=bufA[:, c, 0:NT], in0=mt[:, 0:NT], in1=bufA[:, c, 0:NT], op=ALU.add),
                              reads=mt.r(A) + bufA.r(c), writes=bufA.r(c))
                    yield
            G["s5"] = s5

            def s6():
                wo_b = [wget(("wo", half * 512, (half + 1) * 512)) for half in range(2)]
                for bl in blks:
                    b = b0 + bl
                    xi = rot(xres, "xres")
                    st = rot(stat, "stat")
                    tr.dma("sp", xi[:, :], x_d[(b - 1) * 128:b * 128, :], writes=xi.r(A))
                    pbo = [nbank(), nbank()]
                    for half in range(2):
                        for kc in range(8):
                            mm(pbo[half][:, :], bufA[:, kc, bl * 128:(bl + 1) * 128], wo_b[half][:, kc, :], kc == 0, kc == 7,
                               reads=bufA.r(*range(8)) + wo_b[half].r(A), writes=pbo[half].r(A), signal=(kc == 7))
                    tr.op(DVE, lambda e: e.memset(st[:, 0:2], 0.0), writes=st.r(A))
                    mts = [rot(mtmp, "mtmp"), rot(mtmp, "mtmp")]
                    for half in range(2):
                        tr.op(ACT, lambda e: e.activation(out=mts[half][:, :], in_=pbo[half][:, :], func=AF.Square,
                                                          accum_out=st[:, half:half + 1]),
                              reads=pbo[half].r(A) + st.r(A), writes=mts[half].r(A) + st.r(A))
                    tr.op(DVE, lambda e: e.tensor_tensor(out=st[:, 2:3], in0=st[:, 0:1], in1=st[:, 1:2], op=ALU.add),
                          reads=st.r(A), writes=st.r(A))
                    tr.op(DVE, lambda e: e.tensor_scalar(out=st[:, 3:4], in0=st[:, 2:3], scalar1=1.0 / D, scalar2=EPS,
                                                         op0=ALU.mult, op1=ALU.add), reads=st.r(A), writes=st.r(A))
                    tr.op(ACT, lambda e: e.activation(out=st[:, 4:5], in_=st[:, 3:4], func=AF.Ln), reads=st.r(A), writes=st.r(A))
                    tr.op(ACT, lambda e: e.activation(out=st[:, 5:6], in_=st[:, 4:5], func=AF.Exp, scale=-0.5),
                          reads=st.r(A), writes=st.r(A))
                    for half in range(2):
                        hsl = slice(half * 512, (half + 1) * 512)
                        mt = mts[half]
                        tr.op(DVE, lambda e: e.scalar_tensor_tensor(out=mt[:, :], in0=pbo[half][:, :], scalar=st[:, 5:6],
                                                                    in1=gpost[:, hsl], op0=ALU.mult, op1=ALU.mult),
                              reads=pbo[half].r(A) + st.r(A) + gpost.r(A), writes=mt.r(A))
                        tr.op(ENGS[CFG["s6add"]], lambda e: e.tensor_tensor(out=xi[:, hsl], in0=xi[:, hsl], in1=mt[:, :], op=ALU.add),
                              reads=xi.r(A) + mt.r(A), writes=xi.r(A))
                    tr.dma("sp", out_d[(b - 1) * 128:b * 128, :], xi[:, :], reads=xi.r(A), is_output=True)
                    yield
            G["s6"] = s6
            return G

        groups = _groups()
        GS = [make_group(gi, b0, nb) for gi, (b0, nb) in enumerate(groups)]
        def chain(*gens):
            for g_ in gens:
                yield from g_

        run(GS[0]["s0"]())
        for gi, (b0, nb) in enumerate(groups):
            G = GS[gi]
            run(G["proj1"]())
            if gi == 0:
                G["slide"]()
                run(G["xbc"]())
                run(G["scan"]())
                run(GS[1]["s0"]())
                continue
            if gi == 1:
                wos_prep()
            interleave([(G["attn"](), CFG["w_attn"]), (G["xbc"](), CFG["w_xbc"])])
            interleave([(G["scan"](), CFG["w_scan"]), (G["s3"](), CFG["w_s3"])])
            if gi + 1 < len(groups):
                interleave([(chain(G["s5"](), G["s6"]()), 6), (GS[gi + 1]["s0"](), 4)])
            else:
                run(chain(G["s5"](), G["s6"]()))

        tr.flush()
    return nc, wrec


def _consts():
    ident = np.eye(128, dtype=np.float32)
    tri = np.triu(np.ones((128, 128), np.float32))
    j = np.arange(128)[:, None].astype(np.float64)
    q = np.arange(128)[None, :].astype(np.float64)
    mt = np.zeros((128, 8, 2, 2, 128), np.float64)
    for g in range(4):
        for e in range(2):
            for jj in range(2):
                h = 4 * g + 2 * jj + e
                slope = 2.0 ** (-8.0 * (h + 1) / 16)
                rel_prev = q + 128 - j
                rel_cur = q - j
                mt[:, 2 * g + e, 0, jj, :] = np.where(q < j, -slope * rel_prev, -30000.0)
                mt[:, 2 * g + e, 1, jj, :] = np.where(q >= j, -slope * rel_cur, -30000.0)
    return ident, tri, mt.reshape(128, 8 * 512).astype(np.float32)


def _prep_inputs(x, meta_tokens, g_pre, w_in, conv_w, conv_b, dt_bias, a_log, d_skip, attn_sinks,
                 g_ssm_norm, w_out_att, w_out_ssm, w_out, g_post):
    f = lambda a: np.ascontiguousarray(a, dtype=np.float32)
    w = w_in[0]
    sp = np.cumsum([1024, 256, 256, 1024, 2048, 3072, 32, 1024, 1024])
    wq, wk, wv, wza, wzs, wxbc, wdt, wga, wgs = np.split(w, sp[:-1], axis=1)
    wk2 = np.concatenate([np.concatenate([wk[:, g * 64:(g + 1) * 64]] * 2, axis=1) for g in range(4)], axis=1)
    ident, tri, mtab = _consts()
    sink_tab = np.zeros((128, 8), np.float32)
    for c in range(8):
        sink_tab[0:64, c] = attn_sinks[0, 2 * c]
        sink_tab[64:128, c] = attn_sinks[0, 2 * c + 1]
    dsk_tab = np.zeros((128, 16), np.float32)
    for c in range(16):
        dsk_tab[0:64, c] = d_skip[0, 2 * c]
        dsk_tab[64:128, c] = d_skip[0, 2 * c + 1]
    cw_tab = conv_w[0].reshape(4, 24, 128).transpose(2, 1, 0).reshape(128, 96)
    cb_tab = conv_b[0].reshape(24, 128).T
    shared = {
        "meta": f(meta_tokens), "w_q": f(wq), "w_k2": f(wk2), "w_v": f(wv), "w_za": f(wza), "w_zs": f(wzs),
        "w_xbc": f(wxbc), "w_dt": f(wdt), "w_ga": f(wga), "w_gs": f(wgs),
        "w_oa": f(w_out_att[0]), "w_os": f(w_out_ssm[0]), "w_o": f(w_out[0]),
        "gpre_tab": f(g_pre[0].reshape(8, 128).T), "gpost": f(g_post[0].reshape(1, D)),
        "dt_bias": f(dt_bias[0].reshape(1, 32)), "a_log": f(a_log[0].reshape(1, 32)), "d_skip": f(d_skip[0].reshape(1, 32)),
        "sink_tab": f(sink_tab), "gn_tab": f(g_ssm_norm[0].reshape(16, 128).T), "dsk_tab": f(dsk_tab),
        "cw_tab": f(cw_tab), "cb_tab": f(cb_tab),
        "c_ident": ident, "c_tri": tri, "c_mtab": mtab,
    }
    return shared


def kernel(x, meta_tokens, g_pre, w_in, conv_w, conv_b, dt_bias, a_log, d_skip, attn_sinks,
           g_ssm_norm, w_out_att, w_out_ssm, w_out, g_post):
    x = np.asarray(x, dtype=np.float32)
    shared = _prep_inputs(x, meta_tokens, g_pre, w_in, conv_w, conv_b, dt_bias, a_log, d_skip, attn_sinks,
                          g_ssm_norm, w_out_att, w_out_ssm, w_out, g_post)
    nc, _ = build_nc()
    in_maps = [dict(shared, x=np.ascontiguousarray(x[c])) for c in range(NCORES)]
    res = run_bass_kernel_spmd(nc, in_maps, core_ids=list(range(NCORES)))
    return np.stack([np.asarray(res.results[c]["out"], dtype=np.float32) for c in range(NCORES)], axis=0)
```
